# Optimizing a Trainium2 kernel written in Bass

```python
import math
import jax, jax.numpy as jnp
from jax import lax
import numpy as np

D_MODEL = 1024
BATCH = 32
SEQ = 2048
DEPTH = 2
DEC_BATCH = 16
DEC_SEQ = 2048
PAST_LEN = 128

GRID_W = 64
HEAD_DIM = 64
A_HEADS = D_MODEL // 256
A_V_DIM = 2 * HEAD_DIM
A_WIDTH = A_HEADS * A_V_DIM
B_HEADS = D_MODEL // 128
B_WIDTH = B_HEADS * HEAD_DIM
AB_IN = 3 * A_WIDTH + 3 * B_WIDTH
C_HEADS = D_MODEL // HEAD_DIM
C_KV_HEADS = C_HEADS // 4
C_GROUP = C_HEADS // C_KV_HEADS
C_IN = (C_HEADS + 2 * C_KV_HEADS) * HEAD_DIM
ROPE_AXIS_DIM = HEAD_DIM // 2
ROPE_THETA = 10000.0
NA_ROWS_MAX = 8
NA_COLS = 16
Q_BLOCK = 128
D_FF = 2816
NORM_EPS = 1e-6
N_EVEN = (DEPTH + 1) // 2
N_ODD = DEPTH // 2

kernel_name = 'hybrid_diff_na_axial_gqa_macaron_encoder'


def rms_norm(x, gain):
    xf = x.astype(jnp.float32)
    y = xf * lax.rsqrt(jnp.mean(xf * xf, axis=-1, keepdims=True) + NORM_EPS)
    return (y * gain.astype(jnp.float32)).astype(x.dtype)


def swiglu(x, w_gate, w_up, w_down):
    return (jax.nn.silu(x @ w_gate) * (x @ w_up)) @ w_down


def alibi_slopes(n_heads):
    return jnp.asarray(2.0 ** (-8.0 * np.arange(1, n_heads + 1) / n_heads), dtype=jnp.float32)


def diff_attention(q, k, v, lam):
    bsz, seq = q.shape[0], q.shape[1]
    nb = seq // Q_BLOCK
    scale = HEAD_DIM ** -0.5
    slopes = alibi_slopes(A_HEADS)
    key_pos = jnp.arange(seq)
    qb = jnp.moveaxis(q.reshape(bsz, nb, Q_BLOCK, A_HEADS, 2, HEAD_DIM), 1, 0)

    def one_block(args):
        q_blk, start = args
        s = jnp.einsum('bqhcd,bkhcd->cbhqk', q_blk, k, preferred_element_type=jnp.float32) * scale
        q_pos = start + jnp.arange(Q_BLOCK)
        dist = jnp.abs(q_pos[:, None] - key_pos[None, :]).astype(jnp.float32)
        s = s - slopes[:, None, None] * dist
        p = jax.nn.softmax(s, axis=-1)
        attn = p[0] - lam * p[1]
        return jnp.einsum('bhqk,bkhe->bqhe', attn.astype(v.dtype), v)

    starts = jnp.arange(nb) * Q_BLOCK
    out = lax.map(one_block, (qb, starts))
    return jnp.moveaxis(out, 0, 1).reshape(bsz, seq, A_HEADS, A_V_DIM)


def neighbourhood_attention(q, k, v, rpb):
    bsz, seq, n_heads, dh = q.shape
    rows = seq // GRID_W
    wr = min(NA_ROWS_MAX, rows)
    scale = dh ** -0.5
    qg = jnp.moveaxis(q.reshape(bsz, rows, GRID_W, n_heads, dh), 1, 0)
    kg = k.reshape(bsz, rows, GRID_W, n_heads, dh)
    vg = v.reshape(bsz, rows, GRID_W, n_heads, dh)
    r = jnp.arange(rows)
    row_idx = jnp.clip(r - wr // 2, 0, rows - wr)[:, None] + jnp.arange(wr)[None, :]
    c = jnp.arange(GRID_W)
    col_idx = jnp.clip(c - NA_COLS // 2, 0, GRID_W - NA_COLS)[:, None] + jnp.arange(NA_COLS)[None, :]
    dc_idx = col_idx - c[:, None] + (NA_COLS - 1)

    def one_row(args):
        q_row, rid, r0 = args
        k_nb = kg[:, rid][:, :, col_idx]
        v_nb = vg[:, rid][:, :, col_idx]
        s = jnp.einsum('bchd,brcwhd->bhcrw', q_row, k_nb, preferred_element_type=jnp.float32) * scale
        dr_idx = rid - r0 + (NA_ROWS_MAX - 1)
        bias = rpb[:, dr_idx][:, :, dc_idx]
        s = s + jnp.transpose(bias, (0, 2, 1, 3)).astype(jnp.float32)[None]
        p = jax.nn.softmax(s.reshape(bsz, n_heads, GRID_W, wr * NA_COLS), axis=-1).reshape(s.shape)
        return jnp.einsum('bhcrw,brcwhd->bchd', p.astype(v.dtype), v_nb)

    out = lax.map(one_row, (qg, row_idx, r))
    return jnp.moveaxis(out, 0, 1).reshape(bsz, seq, n_heads, dh)


def mixer_ab(h, w_in, w_out, a_q_norm, a_k_norm, a_lambda_q1, a_lambda_k1, a_lambda_q2, a_lambda_k2,
             a_sub_norm, b_q_norm, b_k_norm, b_rpb, lambda_init):
    bsz, seq, _ = h.shape
    proj = h @ w_in
    aq, ak, av, bq, bk, bv = jnp.split(
        proj, [A_WIDTH, 2 * A_WIDTH, 3 * A_WIDTH, 3 * A_WIDTH + B_WIDTH, 3 * A_WIDTH + 2 * B_WIDTH], axis=-1)
    aq = rms_norm(aq.reshape(bsz, seq, A_HEADS, 2, HEAD_DIM), a_q_norm)
    ak = rms_norm(ak.reshape(bsz, seq, A_HEADS, 2, HEAD_DIM), a_k_norm)
    av = av.reshape(bsz, seq, A_HEADS, A_V_DIM)
    lam = (jnp.exp(jnp.sum(a_lambda_q1.astype(jnp.float32) * a_lambda_k1.astype(jnp.float32)))
           - jnp.exp(jnp.sum(a_lambda_q2.astype(jnp.float32) * a_lambda_k2.astype(jnp.float32)))
           + lambda_init)
    a_out = diff_attention(aq, ak, av, lam)
    a_out = rms_norm(a_out, a_sub_norm) * (1.0 - lambda_init)
    bq = rms_norm(bq.reshape(bsz, seq, B_HEADS, HEAD_DIM), b_q_norm)
    bk = rms_norm(bk.reshape(bsz, seq, B_HEADS, HEAD_DIM), b_k_norm)
    bv = bv.reshape(bsz, seq, B_HEADS, HEAD_DIM)
    b_out = neighbourhood_attention(bq, bk, bv, b_rpb)
    merged = jnp.concatenate([a_out.reshape(bsz, seq, A_WIDTH), b_out.reshape(bsz, seq, B_WIDTH)], axis=-1)
    return merged @ w_out


def axial_rope_angles(seq_len):
    t = jnp.arange(seq_len)
    inv_freq = ROPE_THETA ** (-jnp.arange(0, ROPE_AXIS_DIM, 2, dtype=jnp.float32) / ROPE_AXIS_DIM)
    ang_row = (t // GRID_W).astype(jnp.float32)[:, None] * inv_freq[None, :]
    ang_col = (t % GRID_W).astype(jnp.float32)[:, None] * inv_freq[None, :]
    return ang_row, ang_col


def rotate(x, ang):
    xf = x.astype(jnp.float32)
    half = xf.shape[-1] // 2
    x1, x2 = xf[..., :half], xf[..., half:]
    cos = jnp.cos(ang)[None, :, None, :]
    sin = jnp.sin(ang)[None, :, None, :]
    return jnp.concatenate([x1 * cos - x2 * sin, x2 * cos + x1 * sin], axis=-1).astype(x.dtype)


def apply_axial_rope(x, ang_row, ang_col):
    return jnp.concatenate([rotate(x[..., :ROPE_AXIS_DIM], ang_row),
                            rotate(x[..., ROPE_AXIS_DIM:], ang_col)], axis=-1)


def gqa_attention(q, k, v):
    bsz, seq = q.shape[0], q.shape[1]
    nb = seq // Q_BLOCK
    scale = HEAD_DIM ** -0.5
    qb = jnp.moveaxis(q.reshape(bsz, nb, Q_BLOCK, C_KV_HEADS, C_GROUP, HEAD_DIM), 1, 0)

    def one_block(q_blk):
        s = jnp.einsum('bqkgd,bskd->bkgqs', q_blk, k, preferred_element_type=jnp.float32) * scale
        p = jax.nn.softmax(s, axis=-1)
        return jnp.einsum('bkgqs,bskd->bqkgd', p.astype(v.dtype), v)

    out = lax.map(one_block, qb)
    return jnp.moveaxis(out, 0, 1).reshape(bsz, seq, C_HEADS * HEAD_DIM)


def mixer_c(h, w_in, w_out, q_norm, k_norm):
    bsz, seq, _ = h.shape
    proj = h @ w_in
    q, k, v = jnp.split(proj, [C_HEADS * HEAD_DIM, (C_HEADS + C_KV_HEADS) * HEAD_DIM], axis=-1)
    q = rms_norm(q.reshape(bsz, seq, C_HEADS, HEAD_DIM), q_norm)
    k = rms_norm(k.reshape(bsz, seq, C_KV_HEADS, HEAD_DIM), k_norm)
    v = v.reshape(bsz, seq, C_KV_HEADS, HEAD_DIM)
    ang_row, ang_col = axial_rope_angles(seq)
    q = apply_axial_rope(q, ang_row, ang_col)
    k = apply_axial_rope(k, ang_row, ang_col)
    return gqa_attention(q, k, v) @ w_out


def trunk(x, ffn1_norm, ffn1_w_gate, ffn1_w_up, ffn1_w_down, mix_norm,
          ab_w_in, ab_w_out, a_q_norm, a_k_norm, a_lambda_q1, a_lambda_k1, a_lambda_q2, a_lambda_k2,
          a_sub_norm, b_q_norm, b_k_norm, b_rpb, c_w_in, c_w_out, c_q_norm, c_k_norm,
          ffn2_norm, ffn2_w_gate, ffn2_w_up, ffn2_w_down, final_norm):
    for layer in range(DEPTH):
        x = x + 0.5 * swiglu(rms_norm(x, ffn1_norm[layer]), ffn1_w_gate[layer], ffn1_w_up[layer], ffn1_w_down[layer])
        h = rms_norm(x, mix_norm[layer])
        i = layer // 2
        if layer % 2 == 0:
            lambda_init = 0.8 - 0.6 * math.exp(-0.3 * layer)
            x = x + mixer_ab(h, ab_w_in[i], ab_w_out[i], a_q_norm[i], a_k_norm[i],
                             a_lambda_q1[i], a_lambda_k1[i], a_lambda_q2[i], a_lambda_k2[i],
                             a_sub_norm[i], b_q_norm[i], b_k_norm[i], b_rpb[i], lambda_init)
        else:
            x = x + mixer_c(h, c_w_in[i], c_w_out[i], c_q_norm[i], c_k_norm[i])
        x = x + 0.5 * swiglu(rms_norm(x, ffn2_norm[layer]), ffn2_w_gate[layer], ffn2_w_up[layer], ffn2_w_down[layer])
        x = rms_norm(x, final_norm[layer])
    return x


def setup_inputs(seed: int = 0) -> dict:
    key = jax.random.key(seed)
    keys = jax.random.split(key, 32)

    def nrm(i, shape, scale):
        return scale * jax.random.normal(keys[i], shape, jnp.float32)

    def gain(i, shape):
        return 1.0 + 0.02 * jax.random.normal(keys[i], shape, jnp.float32)

    return {
        'x_prompt': nrm(0, (BATCH, SEQ, D_MODEL), 1.0),
        'x_sample': nrm(1, (DEC_BATCH, DEC_SEQ, D_MODEL), 1.0),
        'ffn1_norm': gain(2, (DEPTH, D_MODEL)),
        'ffn1_w_gate': nrm(3, (DEPTH, D_MODEL, D_FF), D_MODEL ** -0.5),
        'ffn1_w_up': nrm(4, (DEPTH, D_MODEL, D_FF), D_MODEL ** -0.5),
        'ffn1_w_down': nrm(5, (DEPTH, D_FF, D_MODEL), D_FF ** -0.5),
        'mix_norm': gain(6, (DEPTH, D_MODEL)),
        'ab_w_in': nrm(7, (N_EVEN, D_MODEL, AB_IN), D_MODEL ** -0.5),
        'ab_w_out': nrm(8, (N_EVEN, A_WIDTH + B_WIDTH, D_MODEL), (A_WIDTH + B_WIDTH) ** -0.5),
        'a_q_norm': gain(9, (N_EVEN, HEAD_DIM)),
        'a_k_norm': gain(10, (N_EVEN, HEAD_DIM)),
        'a_lambda_q1': nrm(11, (N_EVEN, HEAD_DIM), 0.1),
        'a_lambda_k1': nrm(12, (N_EVEN, HEAD_DIM), 0.1),
        'a_lambda_q2': nrm(13, (N_EVEN, HEAD_DIM), 0.1),
        'a_lambda_k2': nrm(14, (N_EVEN, HEAD_DIM), 0.1),
        'a_sub_norm': gain(15, (N_EVEN, A_V_DIM)),
        'b_q_norm': gain(16, (N_EVEN, HEAD_DIM)),
        'b_k_norm': gain(17, (N_EVEN, HEAD_DIM)),
        'b_rpb': nrm(18, (N_EVEN, B_HEADS, 2 * NA_ROWS_MAX - 1, 2 * NA_COLS - 1), 0.1),
        'c_w_in': nrm(19, (N_ODD, D_MODEL, C_IN), D_MODEL ** -0.5),
        'c_w_out': nrm(20, (N_ODD, C_HEADS * HEAD_DIM, D_MODEL), (C_HEADS * HEAD_DIM) ** -0.5),
        'c_q_norm': gain(21, (N_ODD, HEAD_DIM)),
        'c_k_norm': gain(22, (N_ODD, HEAD_DIM)),
        'ffn2_norm': gain(23, (DEPTH, D_MODEL)),
        'ffn2_w_gate': nrm(24, (DEPTH, D_MODEL, D_FF), D_MODEL ** -0.5),
        'ffn2_w_up': nrm(25, (DEPTH, D_MODEL, D_FF), D_MODEL ** -0.5),
        'ffn2_w_down': nrm(26, (DEPTH, D_FF, D_MODEL), D_FF ** -0.5),
        'final_norm': gain(27, (DEPTH, D_MODEL)),
    }


def reference(x_prompt, x_sample, ffn1_norm, ffn1_w_gate, ffn1_w_up, ffn1_w_down, mix_norm,
              ab_w_in, ab_w_out, a_q_norm, a_k_norm, a_lambda_q1, a_lambda_k1, a_lambda_q2, a_lambda_k2,
              a_sub_norm, b_q_norm, b_k_norm, b_rpb, c_w_in, c_w_out, c_q_norm, c_k_norm,
              ffn2_norm, ffn2_w_gate, ffn2_w_up, ffn2_w_down, final_norm):
    weights = (ffn1_norm, ffn1_w_gate, ffn1_w_up, ffn1_w_down, mix_norm,
               ab_w_in, ab_w_out, a_q_norm, a_k_norm, a_lambda_q1, a_lambda_k1, a_lambda_q2, a_lambda_k2,
               a_sub_norm, b_q_norm, b_k_norm, b_rpb, c_w_in, c_w_out, c_q_norm, c_k_norm,
               ffn2_norm, ffn2_w_gate, ffn2_w_up, ffn2_w_down, final_norm)
    y_prompt = trunk(x_prompt, *weights)
    y_sample = trunk(x_sample, *weights)
    return (y_prompt, y_sample)
```

```python
import math
import os
import numpy as np
from contextlib import ExitStack
import ml_dtypes
import concourse.bass as bass
import concourse.mybir as mybir
from concourse.bass_utils import run_bass_kernel_spmd

F32 = mybir.dt.float32
BF16 = mybir.dt.bfloat16
ALU = mybir.AluOpType
AF = mybir.ActivationFunctionType

D = 1024
S_LEN = 2048
DFF = 2816
NG = 11
EPS = 1e-6
NCORE = 8
NA_NM = 22
NA_TW = NA_NM * 64
DW = 3968
LAMBDA_INIT0 = 0.8 - 0.6 * math.exp(-0.3 * 0)

C_AB_IN = 0
C_AB_OUT = 24
C_CQ = 32
C_CQS = 40
C_CK = 48
C_CKS = 52
C_CV = 56
C_C_OUT = 58
N_ATT = 66

V_FFN1 = 0
V_MIX = 16
V_FFN2 = 32
V_FIN = 48
V_AQ = 64
V_AK = 65
V_BQ = 66
V_BK = 67
V_SUB = 68
V_CQ = 69
V_CQS = 70
V_CK = 71
V_CKS = 72
NVEC = 73


class Op:
    __slots__ = ("eng", "fn", "idx", "signal", "waits", "dma_key", "dma_cnt", "vc", "sigcnt", "dwaits")


class Sched:
    ENGS = ("pe", "act", "dve", "pool", "sp")
    EPOCH = 30000

    def __init__(self, nc):
        self.nc = nc
        self.ops = {e: [] for e in self.ENGS}
        self.last_w = {}
        self.real_w = {}
        self.readers = {}
        self.known = {e: {} for e in self.ENGS}
        self.known_dma = {e: {} for e in self.ENGS}
        self.dma_cnt = {}

    def add(self, eng, fn, reads=(), writes=(), dma=None, record=True):
        op = Op()
        op.eng = eng; op.fn = fn; op.signal = False; op.waits = []; op.dwaits = []
        op.dma_key = dma; op.dma_cnt = 0
        lst = self.ops[eng]
        op.idx = len(lst) + 1
        deps = []
        raw = set()
        ps_reads = [r for r in reads if r.startswith("ps")] if fn is not None else []
        true_writes = writes
        if ps_reads:
            writes = list(writes) + ps_reads
        for r in reads:
            w = self.last_w.get(r)
            if w is not None:
                deps.append(w)
                if self.real_w.get(r) is w:
                    raw.add(id(w))
        for w_ in writes:
            lw = self.last_w.get(w_)
            if lw is not None:
                deps.append(lw)
            rl = self.readers.get(w_)
            if rl:
                deps.extend(rl)
        known = self.known[eng]
        kdma = self.known_dma[eng]
        for y in deps:
            if y is op:
                continue
            if y.dma_key is not None:
                if kdma.get(y.dma_key, 0) >= y.dma_cnt:
                    continue
                kdma[y.dma_key] = y.dma_cnt
                op.dwaits.append((y.dma_key, y.dma_cnt))
                for k, v in y.vc.items():
                    if known.get(k, 0) < v:
                        known[k] = v
                continue
            if y.eng == eng:
                if eng == "pe" or eng == "sp":
                    continue
                if id(y) not in raw:
                    continue
            if known.get(y.eng, 0) >= y.idx:
                continue
            y.signal = True
            op.waits.append(y)
            for k, v in y.vc.items():
                if known.get(k, 0) < v:
                    known[k] = v
            known[y.eng] = y.idx
        op.vc = dict(known)
        if dma is not None:
            c = self.dma_cnt.get(dma, 0) + 16
            self.dma_cnt[dma] = c
            op.dma_cnt = c
        if record and fn is not None:
            for r in reads:
                self.readers.setdefault(r, []).append(op)
            for w_ in writes:
                self.last_w[w_] = op
                self.readers[w_] = []
            for w_ in true_writes:
                self.real_w[w_] = op
        lst.append(op)
        return op

    def fence(self, keys):
        keys = list(keys)
        for e in self.ENGS:
            self.add(e, None, reads=keys, writes=keys, record=False)

    def emit(self, stack):
        nc = self.nc
        sems = {}

        def sem(name):
            if name not in sems:
                sems[name] = stack.enter_context(nc.semaphore("s_" + name.replace(":", "_")))
            return sems[name]
        for e in self.ENGS:
            c = 0
            for op in self.ops[e]:
                if op.signal:
                    c += 1
                    op.sigcnt = c
        for e in self.ENGS:
            for op in self.ops[e]:
                if op.signal:
                    sem("%s:%d" % (e, (op.sigcnt - 1) // self.EPOCH))
                if op.dma_key is not None:
                    sem("d:" + str(op.dma_key))
        block = stack.enter_context(nc.Block())
        EP = self.EPOCH

        def run(engobj, e):
            for op in self.ops[e]:
                for y in op.waits:
                    ep = (y.sigcnt - 1) // EP
                    engobj.wait_ge(sem("%s:%d" % (y.eng, ep)), y.sigcnt - ep * EP)
                for key, cnt in op.dwaits:
                    engobj.wait_ge(sem("d:" + str(key)), cnt)
                if op.fn is None:
                    continue
                ins = op.fn(engobj)
                if op.dma_key is not None:
                    ins.then_inc(sem("d:" + str(op.dma_key)), 16)
                elif op.signal:
                    ep = (op.sigcnt - 1) // EP
                    ins.then_inc(sem("%s:%d" % (e, ep)), 1)

        @block.tensor
        def _(eng): run(eng, "pe")

        @block.scalar
        def _(eng): run(eng, "act")

        @block.vector
        def _(eng): run(eng, "dve")

        @block.gpsimd
        def _(eng): run(eng, "pool")

        @block.sync
        def _(eng): run(eng, "sp")


_ALIBI_ZERO = []


def _alibi_zero():
    if not _ALIBI_ZERO:
        al = np.asarray(_consts()[2]).astype(np.float32).reshape(4, 128, DW)
        z = np.zeros((4, 4, 16), bool)
        for h in range(4):
            for qb in range(4):
                for kc in range(16):
                    m0 = qb * 512 - 128 * kc + 1920
                    z[h, qb, kc] = not np.any(al[h][:, m0:m0 + 512])
        _ALIBI_ZERO.append(z)
    return _ALIBI_ZERO[0]


def build(nseq, stop=99):
    ALIBI_ZERO = _alibi_zero()
    nc = bass.Bass("TRN2", target_bir_lowering=False)
    dt = lambda n, s, d, k: nc.dram_tensor(n, s, d, kind=k).ap()
    xin = dt("xin", [nseq * S_LEN, D], F32, "ExternalInput")
    yout = dt("yout", [nseq * S_LEN, D], F32, "ExternalOutput")
    wffn32 = dt("wffn", [4 * NG * 128, 6144], F32, "ExternalInput")
    watt32 = dt("watt", [N_ATT * 128, 1024], F32, "ExternalInput")
    vec_d = dt("vec", [128, NVEC], F32, "ExternalInput")
    lamv_d = dt("lamv", [128, 256], F32, "ExternalInput")
    identf_d = dt("identf", [128, 128], F32, "ExternalInput")
    cmat_d = dt("cmat", [128, 512], BF16, "ExternalInput")
    alibi_d = dt("alibi", [4 * 128, DW], BF16, "ExternalInput")
    rope_d = dt("rope", [128, 4096], F32, "ExternalInput")
    rpbt_d = dt("rpbt", [8 * 128, NA_TW], F32, "ExternalInput")
    navalid_d = dt("navalid", [128, 2 * NA_TW], BF16, "ExternalInput")
    wffn = dt("wffn_b", [4 * NG * 128, 6144], BF16, "Internal")
    watt = dt("watt_b", [N_ATT * 128, 1024], BF16, "Internal")
    natab = dt("natab", [8 * 128, 2 * NA_TW], BF16, "Internal")

    st = ExitStack()
    with st:
        sb = lambda n, s, d: st.enter_context(nc.sbuf_tensor(n, s, d))
        xT = sb("xT", [128, 8, S_LEN], F32)
        xn = sb("xn", [128, 8, S_LEN], BF16)
        R = sb("R", [128, 20480], BF16)
        TMP = sb("TMP", [128, 8, 512], F32)
        SQ = sb("SQ", [128, 4, 512], BF16)
        AH = sb("AH", [128, 4, S_LEN], BF16)
        AW = sb("AW", [128, 6, 1024], BF16)
        TBL = sb("TBL", [128, 5120], F32)
        identf = sb("identf_s", [128, 128], F32)
        cmat = sb("cmat_s", [128, 512], BF16)
        vec = sb("vec_s", [128, NVEC + 8], F32)
        lamt = sb("lamt", [128, 256], F32)
        lam2 = sb("lam2", [128, 8], F32)
        PSA = st.enter_context(nc.psum_tensor("psa", [128, 4096], F32))
        PS = [PSA[:, i * 512:(i + 1) * 512] for i in range(8)]
        S = Sched(nc)

        ones1024 = cmat[:, 0:128]
        blk64 = cmat[:, 128:256]
        ones128 = cmat[:, 256:384]
        ones1 = cmat[:, 384:512]
        eps_ap = vec[:, NVEC:NVEC + 1]
        V_GQ8 = NVEC + 1
        neglam = lam2[:, 4:5]
        V_SUB8 = NVEC + 5

        TBLb = TBL[:].bitcast(BF16)
        Dtab = TBLb[:, 0:DW]
        NAtab = TBLb[:, 4096:4096 + 4 * NA_TW]
        ropeC = TBL[:, 0:2048]
        ropeS = TBL[:, 2048:4096]
        Rf = R[:].bitcast(F32)

        psi = [0]

        def psbank(lo=0, hi=8):
            b = lo + psi[0] % (hi - lo)
            psi[0] += 1
            return b

        def mm(b, ncol, lhsT, rhs, start, stop, reads, col0=0, prow=None):
            out = PS[b][:, col0:col0 + ncol] if prow is None else PS[b][prow[0]:prow[1], col0:col0 + ncol]
            S.add("pe", lambda e, o=out, l=lhsT, r=rhs, s=start, p=stop: e.matmul(o, l, r, start=s, stop=p),
                  reads=reads, writes=["ps%d" % b])

        def act(out, in_, func, reads, writes, bias=None, scale=None):
            kw = {}
            if bias is not None:
                kw["bias"] = bias
            if scale is not None:
                kw["scale"] = scale
            S.add("act", lambda e, o=out, i=in_, f=func, kw=kw: e.activation(out=o, in_=i, func=f, **kw),
                  reads=reads, writes=writes)

        def tt(eng, out, in0, in1, op, reads, writes):
            S.add(eng, lambda e, o=out, a=in0, b=in1, p=op: e.tensor_tensor(out=o, in0=a, in1=b, op=p),
                  reads=reads, writes=writes)

        def stt(out, in0, scalar, in1, op0, op1, reads, writes):
            S.add("dve", lambda e, o=out, a=in0, s=scalar, b=in1, p0=op0, p1=op1:
                  e.scalar_tensor_tensor(out=o, in0=a, scalar=s, in1=b, op0=p0, op1=p1),
                  reads=reads, writes=writes)

        def dma(q, out, in_, reads, writes, key):
            S.add(q, lambda e, o=out, i=in_: e.dma_start(out=o, in_=i), reads=reads, writes=writes, dma=key)

        tmpi = [0]

        def tmp():
            i = tmpi[0] % 8
            tmpi[0] += 1
            return i

        sqi = [0]

        def sqslot():
            i = sqi[0] % 4
            sqi[0] += 1
            return i

        def rstd_from_psum(b, ncol, extra_reads=()):
            t = tmp()
            act(TMP[:, t, 0:ncol], PS[b][:, 0:ncol], AF.Ln, ["ps%d" % b, "vec"], ["tmp%d" % t], bias=eps_ap, scale=1.0)
            act(TMP[:, t, 0:ncol], TMP[:, t, 0:ncol], AF.Exp, ["tmp%d" % t], ["tmp%d" % t], scale=-0.5)
            return t

        dma("sp", identf[:], identf_d, [], ["identf"], "c_id")
        dma("sp", cmat[:], cmat_d, [], ["cmat"], "c_cm")
        dma("sp", vec[:, 0:NVEC], vec_d, [], ["vec"], "c_vec")
        dma("sp", lamt[:], lamv_d, [], ["lamt"], "c_lam")
        S.add("pool", lambda e: e.memset(vec[:, NVEC:NVEC + 1], EPS), reads=["vec"], writes=["vec"])
        for i, c in enumerate((V_AQ, V_BQ, V_CQ, V_CQS)):
            S.add("dve", lambda e, i=i, c=c: e.tensor_scalar(out=vec[:, V_GQ8 + i:V_GQ8 + i + 1], in0=vec[:, c:c + 1],
                                                            scalar1=0.125, scalar2=None, op0=ALU.mult),
                  reads=["vec"], writes=["vec"])
        S.add("dve", lambda e: e.tensor_scalar(out=vec[:, V_SUB8:V_SUB8 + 1], in0=vec[:, V_SUB:V_SUB + 1],
                                               scalar1=1.0 - LAMBDA_INIT0, scalar2=None, op0=ALU.mult),
              reads=["vec"], writes=["vec"])
        S.add("dve", lambda e: e.tensor_tensor(out=lamt[:, 0:64], in0=lamt[:, 0:64], in1=lamt[:, 64:128], op=ALU.mult),
              reads=["lamt"], writes=["lamt"])
        S.add("dve", lambda e: e.tensor_tensor(out=lamt[:, 128:192], in0=lamt[:, 128:192], in1=lamt[:, 192:256], op=ALU.mult),
              reads=["lamt"], writes=["lamt"])
        S.add("dve", lambda e: e.reduce_sum(out=lam2[:, 0:1], in_=lamt[:, 0:64], axis=mybir.AxisListType.X),
              reads=["lamt"], writes=["lam2"])
        S.add("dve", lambda e: e.reduce_sum(out=lam2[:, 1:2], in_=lamt[:, 128:192], axis=mybir.AxisListType.X),
              reads=["lamt", "lam2"], writes=["lam2"])
        act(lam2[:, 2:4], lam2[:, 0:2], AF.Exp, ["lam2"], ["lam2"])
        S.add("dve", lambda e: e.scalar_tensor_tensor(out=lam2[:, 4:5], in0=lam2[:, 3:4], scalar=-LAMBDA_INIT0, in1=lam2[:, 2:3],
                                                      op0=ALU.add, op1=ALU.subtract),
              reads=["lam2"], writes=["lam2"])
        def cast_ffn(fi):
            for i in range(fi * NG, (fi + 1) * NG):
                key = "wffn0_g%d" % (i - fi * NG) if fi == 0 else "wffn%d" % fi
                sem_ = "cw0_%d" % (i - fi * NG) if fi == 0 else "cw%d" % fi
                dma("pool", wffn[i * 128:(i + 1) * 128, :], wffn32[i * 128:(i + 1) * 128, :], [], [key], sem_)
        cast_ffn(0)
        for i in range(0, N_ATT, 6):
            dma("pool", watt[i * 128:(i + 6) * 128, :], watt32[i * 128:(i + 6) * 128, :], [], ["watt"], "ca")
        nav = R[:, 0:2 * NA_TW]
        dma("sp", nav, navalid_d, [], ["nav"], "c_nav")
        for h in range(8):
            dma("sp", TBL[:, 0:NA_TW], rpbt_d[h * 128:(h + 1) * 128, :], ["tbl_e"], ["tbl_r"], "c_rp")
            act(TBL[:, 2048:2048 + NA_TW], TBL[:, 0:NA_TW], AF.Exp, ["tbl_r"], ["tbl_e"])
            ob = TBLb[:, 7168:7168 + 2 * NA_TW]
            for k in range(2):
                tt("dve", ob[:, k * NA_TW:(k + 1) * NA_TW], TBL[:, 2048:2048 + NA_TW], nav[:, k * NA_TW:(k + 1) * NA_TW],
                   ALU.mult, ["tbl_e", "nav", "tbl_o"], ["tbl_o"])
            dma("sp", natab[h * 128:(h + 1) * 128, :], ob, ["tbl_o"], ["natab"], "c_nt")
        S.fence(["nav", "tbl_r", "tbl_e", "tbl_o", "R", "TBL"])

        XK = ["xT%d_%d" % (c, tb) for c in range(8) for tb in range(4)]
        xk = lambda c, tb: "xT%d_%d" % (c, tb)
        nk = lambda c, tb: "xn%d_%d" % (c, tb)

        def in_tile(s, tt_, q="sp", base=0, pfx="stg", spfx="ld"):
            slot = tt_ % 2
            stg = Rf[:, base + slot * 1024:base + (slot + 1) * 1024]
            dma(q, stg, xin[(s * 16 + tt_) * 128:(s * 16 + tt_ + 1) * 128, :], [], ["%s%d" % (pfx, slot)], "%s%d" % (spfx, slot))
            for half in range(2):
                b = psbank()
                for c4 in range(4):
                    c = half * 4 + c4
                    S.add("pe", lambda e, b=b, c4=c4, c=c, stg=stg: e.transpose(PS[b][:, c4 * 128:(c4 + 1) * 128],
                                                                              stg[:, c * 128:(c + 1) * 128], identf[:]),
                          reads=["%s%d" % (pfx, slot), "identf"], writes=["ps%d" % b])
                tb = tt_ // 4
                o = xT[:, half * 4:half * 4 + 4, tt_ * 128:(tt_ + 1) * 128]
                i = PS[b][:, :].rearrange("p (c t) -> p c t", c=4)
                wr = [xk(half * 4 + c4, tb) for c4 in range(4)]
                if half == 0:
                    S.add("dve", lambda e, o=o, i=i: e.tensor_copy(out=o, in_=i), reads=["ps%d" % b], writes=wr)
                else:
                    act(o, i, AF.Copy, ["ps%d" % b], wr)

        def out_tile(s, tt_):
            slot = tt_ % 2
            stg = Rf[:, slot * 1024:(slot + 1) * 1024]
            tb = tt_ // 4
            for half in range(2):
                b = psbank()
                for c4 in range(4):
                    c = half * 4 + c4
                    S.add("pe", lambda e, b=b, c4=c4, c=c, tt_=tt_: e.transpose(PS[b][:, c4 * 128:(c4 + 1) * 128],
                                                                      xT[:, c, tt_ * 128:(tt_ + 1) * 128], identf[:]),
                          reads=[xk(c, tb), "identf"], writes=["ps%d" % b])
                o = stg[:, half * 512:(half + 1) * 512]
                if half == 0:
                    S.add("dve", lambda e, o=o, b=b: e.tensor_copy(out=o, in_=PS[b][:, :]), reads=["ps%d" % b],
                          writes=["stg%d" % slot])
                else:
                    act(o, PS[b][:, :], AF.Copy, ["ps%d" % b], ["stg%d" % slot])
            dma("sp", yout[(s * 16 + tt_) * 128:(s * 16 + tt_ + 1) * 128, :], stg, ["stg%d" % slot], ["yout%d" % slot], "st%d" % slot)

        def phase_in(s):
            for tt_ in range(16):
                in_tile(s, tt_)

        def phase_out(s):
            for tt_ in range(16):
                out_tile(s, tt_)

        def phase_out_in(s):
            for i_ in range(20):
                if i_ < 16:
                    out_tile(s, i_)
                if i_ >= 4:
                    in_tile(s + 1, i_ - 4, q="pool", base=2048, pfx="stgi", spfx="ldi")

        def norm_stats(tb):
            cols = slice(tb * 512, (tb + 1) * 512)
            b = psbank()
            for c in range(8):
                q = sqslot()
                if c % 2 == 0:
                    act(SQ[:, q, :], xT[:, c, cols], AF.Square, [xk(c, tb)], ["sq%d" % q])
                else:
                    tt("pool", SQ[:, q, :], xT[:, c, cols], xT[:, c, cols], ALU.mult, [xk(c, tb)], ["sq%d" % q])
                mm(b, 512, ones1024, SQ[:, q, :], c == 0, c == 7, ["sq%d" % q, "cmat"])
            return rstd_from_psum(b, 512)

        def norm_apply(gcol, inplace, tb, t):
            cols = slice(tb * 512, (tb + 1) * 512)
            for c in range(8):
                g = vec[:, gcol + c:gcol + c + 1]
                if inplace:
                    stt(xT[:, c, cols], xT[:, c, cols], g, TMP[:, t, :], ALU.mult, ALU.mult,
                        [xk(c, tb), "tmp%d" % t, "vec"], [xk(c, tb)])
                else:
                    stt(xn[:, c, cols], xT[:, c, cols], g, TMP[:, t, :], ALU.mult, ALU.mult,
                        [xk(c, tb), "tmp%d" % t, "vec"], [nk(c, tb)])

        def phase_norm(gcol, inplace):
            for tb in range(4):
                t = norm_stats(tb)
                norm_apply(gcol, inplace, tb, t)

        def norm_chain(specs):
            def make(tb):
                hold = [None]
                stages = []

                def st_first():
                    hold[0] = norm_stats(tb)
                stages.append(st_first)
                for i, (gcol, inplace) in enumerate(specs):
                    def st_(i=i, gcol=gcol, inplace=inplace):
                        norm_apply(gcol, inplace, tb, hold[0])
                        if i + 1 < len(specs):
                            hold[0] = norm_stats(tb)
                    stages.append(st_)
                return stages
            return make

        def phase_ffn(fi, chain=None, preloaded=False):
            gus = lambda slot: R[:, slot * 4096:(slot + 1) * 4096]
            dws = lambda slot: R[:, 8192 + slot * 2048:8192 + (slot + 1) * 2048]
            ab = lambda slot: R[:, 12288 + slot * 4096:12288 + (slot + 1) * 4096]

            wkey = lambda g: ("wffn0_g%d" % g) if fi == 0 else ("wffn%d" % fi)

            def load_gu(g):
                slot = g % 2
                rows = slice((fi * NG + g) * 128, (fi * NG + g + 1) * 128)
                dma("sp", gus(slot), wffn[rows, 0:4096], [wkey(g)], ["gu%d" % slot], "gu%d" % slot)

            def load_dw(g):
                slot = g % 2
                rows = slice((fi * NG + g) * 128, (fi * NG + g + 1) * 128)
                dma("sp", dws(slot), wffn[rows, 4096:6144], [wkey(g)], ["dw%d" % slot], "dw%d" % slot)

            def up_step(g, tb, j):
                slot = g % 2
                w = gus(slot)
                a = ab(slot)
                cols = slice(tb * 512, (tb + 1) * 512)
                bg = psbank()
                for kc in range(8):
                    mm(bg, 512, w[:, kc * 256 + j * 128:kc * 256 + (j + 1) * 128], xn[:, kc, cols], kc == 0, kc == 7,
                       ["gu%d" % slot, nk(kc, tb)])
                bu = psbank()
                for kc in range(8):
                    mm(bu, 512, w[:, 2048 + kc * 256 + j * 128:2048 + kc * 256 + (j + 1) * 128], xn[:, kc, cols],
                       kc == 0, kc == 7, ["gu%d" % slot, nk(kc, tb)])
                t = tmp()
                act(TMP[:, t, :], PS[bg][:, :], AF.Silu, ["ps%d" % bg], ["tmp%d" % t])
                tt("dve", a[:, j * 2048 + tb * 512:j * 2048 + (tb + 1) * 512], PS[bu][:, :], TMP[:, t, :], ALU.mult,
                   ["ps%d" % bu, "tmp%d" % t], ["act%d_%d_%d" % (slot, j, tb)])

            def down_step(g, tb, oc):
                slot = g % 2
                w = dws(slot)
                a = ab(slot)
                cols = slice(tb * 512, (tb + 1) * 512)
                b = psbank()
                for j in range(2):
                    mm(b, 512, w[:, j * 1024 + oc * 128:j * 1024 + (oc + 1) * 128],
                       a[:, j * 2048 + tb * 512:j * 2048 + (tb + 1) * 512], j == 0, j == 1,
                       ["dw%d" % slot, "act%d_%d_%d" % (slot, j, tb)])
                stt(xT[:, oc, cols], PS[b][:, :], 0.5, xT[:, oc, cols], ALU.mult, ALU.add,
                    ["ps%d" % b, xk(oc, tb)], [xk(oc, tb)])
            if fi == "prefetch":
                return load_gu, load_dw
            if not preloaded:
                load_gu(0)
                load_dw(0)
            pending = []
            for g in range(NG + 1):
                if g + 1 < NG:
                    load_gu(g + 1)
                for st_ in range(8):
                    if g < NG:
                        up_step(g, st_ // 2, st_ % 2)
                    if g >= 1:
                        for k in range(4):
                            down_step(g - 1, st_ // 2, (st_ % 2) * 4 + k)
                    if g == NG and chain is not None and st_ % 2 == 1:
                        pending.append(chain(st_ // 2))
                        for stg_ in pending:
                            if stg_:
                                stg_.pop(0)()
                if g + 1 < NG:
                    load_dw(g + 1)
            while any(pending):
                for stg_ in pending:
                    if stg_:
                        stg_.pop(0)()

        awi = [0]

        def load_w(chunk):
            slot = awi[0] % 6
            awi[0] += 1
            dma("sp", AW[:, slot, :], watt[chunk * 128:(chunk + 1) * 128, :], ["watt"], ["aw%d" % slot], "aw%d" % slot)
            return slot

        def proj_fm(slot, tb, ncols=128, c0=0):
            b = psbank()
            for kc in range(8):
                mm(b, 512, AW[:, slot, kc * 128 + c0:kc * 128 + c0 + ncols], xn[:, kc, tb * 512:(tb + 1) * 512], kc == 0, kc == 7,
                   ["aw%d" % slot, nk(kc, tb)], prow=(0, ncols))
            return b

        def headnorm_rstd(b):
            q = sqslot()
            act(SQ[:, q, :], PS[b][:, :], AF.Square, ["ps%d" % b], ["sq%d" % q])
            b2 = psbank()
            mm(b2, 512, blk64, SQ[:, q, :], True, True, ["sq%d" % q, "cmat"])
            return rstd_from_psum(b2, 512)

        def proj_qk_norm(chunk, gcolumn, dst, dkey):
            slot = load_w(chunk)
            for tb in range(4):
                b = proj_fm(slot, tb)
                t = headnorm_rstd(b)
                stt(dst[:, tb * 512:(tb + 1) * 512], PS[b][:, :], vec[:, gcolumn:gcolumn + 1], TMP[:, t, :], ALU.mult, ALU.mult,
                    ["ps%d" % b, "tmp%d" % t, "vec"], ["%s_%d" % (dkey, tb)])

        def proj_v_tm(chunk, c0, ncols, dst_fn, dkey):
            slot = load_w(chunk)
            per = 512 // ncols
            for t0 in range(0, 16, per):
                b = psbank()
                for i in range(per):
                    tt_ = t0 + i
                    for kc in range(8):
                        mm(b, ncols, xn[:, kc, tt_ * 128:(tt_ + 1) * 128], AW[:, slot, kc * 128 + c0:kc * 128 + c0 + ncols],
                           kc == 0, kc == 7, ["aw%d" % slot, nk(kc, tt_ // 4)], col0=i * ncols)
                for i in range(per):
                    tt_ = t0 + i
                    o = dst_fn(tt_)
                    src = PS[b][:, i * ncols:(i + 1) * ncols]
                    if i % 2 == 0:
                        S.add("dve", lambda e, o=o, src=src: e.tensor_copy(out=o, in_=src), reads=["ps%d" % b],
                              writes=["%s_%d" % (dkey, tt_ // 4)])
                    else:
                        act(o, src, AF.Copy, ["ps%d" % b], ["%s_%d" % (dkey, tt_ // 4)])

        qslot = lambda i: R[:, i * 2048:(i + 1) * 2048]
        kslot = lambda i: R[:, 4096 + i * 2048:4096 + (i + 1) * 2048]
        vslot = lambda i: R[:, 8192 + i * 3072:8192 + (i + 1) * 3072].rearrange("p (t c) -> p t c", c=192)
        eslot = lambda i: R[:, 14336 + i * 512:14336 + (i + 1) * 512]
        pslot = lambda i: R[:, 16384 + i * 512:16384 + (i + 1) * 512]
        epi = [0, 0]

        def attn_core(qs, ks, vkey, blocks, mask_fn, pv, fin, accsets, skip_fn=None, nsp=2, lay=None):
            qT = qslot(qs); kT = kslot(ks)
            flat = []
            for bi, (q0, nq, kcs, tabsel) in enumerate(blocks):
                kl = [kc for kc in kcs if not (skip_fn is not None and skip_fn(q0, kc, nq))]
                for i, kc in enumerate(kl):
                    flat.append((bi, q0, nq, kc, tabsel, i == 0, i == len(kl) - 1))
            sb_of = {}

            def qk(n):
                bi, q0, nq, kc, tabsel, first, last = flat[n]
                sbk = [2 * (n % nsp), 2 * (n % nsp) + 1]
                for side in range(2):
                    if lay is None:
                        pr = slice(side * 64, (side + 1) * 64)
                        mm(sbk[side], nq, kT[pr, kc * 128:(kc + 1) * 128], qT[pr, q0:q0 + nq], True, True,
                           ["k%d_%d" % (ks, kc // 4), "q%d_%d" % (qs, q0 // 512)])
                    else:
                        mm(sbk[side], nq, lay["k"][:, kc * 128:(kc + 1) * 128], lay["q"][side][:, q0:q0 + nq], True, True,
                           ["k%d_%d" % (ks, kc // 4), "q%d_%d" % (qs, q0 // 512)])
                sb_of[n] = sbk
            LA = nsp
            for n0 in range(min(LA, len(flat))):
                qk(n0)
            pend = []
            PSL = (16384, 17408, 19456) if lay is None else lay["P"]
            ESL = (14336, 15360, 18432) if lay is None else lay["E"]
            for n in range(len(flat)):
                bi, q0, nq, kc, tabsel, first, last = flat[n]
                banks = accsets[bi % len(accsets)]
                sbk = sb_of.pop(n)
                pp = epi[1] % len(PSL); epi[1] += 1
                PP = R[:, PSL[pp]:PSL[pp] + 1024].rearrange("p (s q) -> p s q", s=2)
                SS = PSA[:, sbk[0] * 512:(sbk[0] + 2) * 512].rearrange("p (s q) -> p s q", s=2)
                skeys = ["ps%d" % sbk[0], "ps%d" % sbk[1]]
                if mask_fn is None:
                    act(PP[:, :, 0:nq], SS[:, :, 0:nq], AF.Exp, skeys, ["pp%d" % pp])
                else:
                    ep = epi[0] % len(ESL); epi[0] += 1
                    EE = R[:, ESL[ep]:ESL[ep] + 1024].rearrange("p (s q) -> p s q", s=2)
                    act(EE[:, :, 0:nq], SS[:, :, 0:nq], AF.Exp, skeys, ["ee%d" % ep])
                    tt("dve", PP[:, :, 0:nq], EE[:, :, 0:nq], mask_fn(tabsel, q0, kc, nq), ALU.mult, ["ee%d" % ep, "mask"], ["pp%d" % pp])
                used_fb = False
                if pend:
                    for f_ in pend.pop(0):
                        used_fb = bool(f_(sbk[0])) or used_fb
                if last:
                    while pend:
                        for f_ in pend.pop(0):
                            f_(sbk[0])
                if n + LA < len(flat) and not used_fb:
                    qk(n + LA)
                for side in range(2):
                    for (ai, lfn, xr) in pv[side]:
                        mm(banks[ai], nq, lfn(kc), PP[:, side, 0:nq], first, last,
                           ["pp%d" % pp, "%s_%d" % (vkey, kc // 4)] + xr)
                if n + LA < len(flat) and used_fb:
                    qk(n + LA)
                if last:
                    stages = fin(q0, nq, banks, sbk[0])
                    for i_, stg_ in enumerate(stages):
                        if i_ < len(pend):
                            pend[i_].extend(stg_)
                        else:
                            pend.append(list(stg_))
            for lst_ in pend:
                for f_ in lst_:
                    f_(0)

        def out_proj(c_out0, half, nkc=4):
            for oc in range(8):
                slot = load_w(c_out0 + oc)
                for tb in range(4):
                    b = psbank()
                    for kc in range(nkc):
                        kk = half * 4 + kc
                        mm(b, 512, AW[:, slot, kk * 128:(kk + 1) * 128], AH[:, kc, tb * 512:(tb + 1) * 512], kc == 0, kc == nkc - 1,
                           ["aw%d" % slot, "ah%d_%d" % (kc, tb)])
                    stt(xT[:, oc, tb * 512:(tb + 1) * 512], PS[b][:, :], 1.0, xT[:, oc, tb * 512:(tb + 1) * 512], ALU.mult, ALU.add,
                        ["ps%d" % b, xk(oc, tb)], [xk(oc, tb)])

        def recip(t, nq, scr):
            S.add("dve", lambda e, t=t, nq=nq: e.reciprocal(out=TMP[:, t, 0:nq], in_=TMP[:, t, 0:nq]),
                  reads=["tmp%d" % t], writes=["tmp%d" % t])

        def fin_pair(cj):
            def fin(q0, nq, banks, fb):
                t = tmp()
                tbq = q0 // 512

                def s0(fb_):
                    act(TMP[64:128, t, 0:nq], PS[banks[0]][64:128, 0:nq], AF.Copy, ["ps%d" % banks[0]], ["tmp%d" % t])
                    act(TMP[0:64, t, 0:nq], PS[banks[1]][0:64, 0:nq], AF.Copy, ["ps%d" % banks[1]], ["tmp%d" % t])

                def s1(fb_):
                    recip(t, nq, None)

                def s2(fb_):
                    tt("dve", AH[0:64, cj, q0:q0 + nq], PS[banks[0]][0:64, 0:nq], TMP[64:128, t, 0:nq], ALU.mult,
                       ["ps%d" % banks[0], "tmp%d" % t], ["ah%d_%d" % (cj, tbq)])
                    tt("dve", AH[64:128, cj, q0:q0 + nq], PS[banks[1]][64:128, 0:nq], TMP[0:64, t, 0:nq], ALU.mult,
                       ["ps%d" % banks[1], "tmp%d" % t], ["ah%d_%d" % (cj, tbq)])
                return [[s0], [s1], [s2]]
            return fin

        def fin_pair1(cj, act_recip=False):
            def fin(q0, nq, banks, fb):
                tr, to = tmp(), tmp()
                tbq = q0 // 512
                f0 = AF.Ln if act_recip else AF.Copy

                def s0(fb_):
                    act(TMP[0:64, tr, 0:nq], PS[banks[0]][64:128, 0:nq], f0, ["ps%d" % banks[0]], ["tmp%d" % tr])
                    act(TMP[64:128, tr, 0:nq], PS[banks[1]][0:64, 0:nq], f0, ["ps%d" % banks[1]], ["tmp%d" % tr])
                    S.add("dve", lambda e: e.tensor_copy(out=TMP[0:64, to, 0:nq], in_=PS[banks[0]][0:64, 0:nq]),
                          reads=["ps%d" % banks[0]], writes=["tmp%d" % to])
                    S.add("dve", lambda e: e.tensor_copy(out=TMP[64:128, to, 0:nq], in_=PS[banks[1]][64:128, 0:nq]),
                          reads=["ps%d" % banks[1]], writes=["tmp%d" % to])

                def s1(fb_):
                    if act_recip:
                        act(TMP[:, tr, 0:nq], TMP[:, tr, 0:nq], AF.Exp, ["tmp%d" % tr], ["tmp%d" % tr], scale=-1.0)
                    else:
                        recip(tr, nq, None)

                def s2(fb_):
                    tt("dve", AH[:, cj, q0:q0 + nq], TMP[:, to, 0:nq], TMP[:, tr, 0:nq], ALU.mult,
                       ["tmp%d" % to, "tmp%d" % tr], ["ah%d_%d" % (cj, tbq)])
                return [[s0], [s1], [s2]]
            return fin

        FULLB = [(qb * 512, 512, list(range(16)), 0) for qb in range(4)]

        def phase_mix0():
            RK = ["R"]
            S.fence(RK + ["gu0", "gu1", "dw0", "dw1"] + ["act%d_%d_%d" % (s_, j, tb) for s_ in range(2) for j in range(2) for tb in range(4)])
            qda = lambda i: R[:, i * 4096:i * 4096 + 2048]
            qdb = lambda i: R[:, i * 4096 + 2048:(i + 1) * 4096]
            kd = lambda i: R[:, 8192 + i * 2048:8192 + (i + 1) * 2048]
            vd = lambda i: R[:, 12288 + i * 2048:12288 + (i + 1) * 2048]
            for i in range(2):
                S.add("pool", lambda e, i=i: e.memset(qda(i)[64:128, :], 0.0), reads=[], writes=["q%d_%d" % (i, t4) for t4 in range(4)])
                S.add("pool", lambda e, i=i: e.memset(qdb(i)[0:64, :], 0.0), reads=[], writes=["q%d_%d" % (i, t4) for t4 in range(4)])
            for h in range(4):
                qs = ks = vs = h % 2
                slot_q = load_w(C_AB_IN + h)
                for tb in range(4):
                    cols = slice(tb * 512, (tb + 1) * 512)
                    b = proj_fm(slot_q, tb)
                    t = headnorm_rstd(b)
                    gq = vec[:, V_GQ8 + 0:V_GQ8 + 1]
                    stt(qda(qs)[0:64, cols], PS[b][0:64, :], gq[0:64, :], TMP[0:64, t, :], ALU.mult, ALU.mult,
                        ["ps%d" % b, "tmp%d" % t, "vec"], ["q%d_%d" % (qs, tb)])
                    stt(qdb(qs)[64:128, cols], PS[b][64:128, :], gq[64:128, :], TMP[64:128, t, :], ALU.mult, ALU.mult,
                        ["ps%d" % b, "tmp%d" % t, "vec"], ["q%d_%d" % (qs, tb)])
                proj_qk_norm(C_AB_IN + 4 + h, V_AK, kd(ks), "k%d" % ks)
                proj_v_tm(C_AB_IN + 8 + h, 0, 128, lambda tt_, vs=vs: vd(vs)[:, tt_ * 128:(tt_ + 1) * 128], "v%d" % vs)
                dma("sp", Dtab, alibi_d[h * 128:(h + 1) * 128, :], ["mask"], ["mask"], "dtab")
                lay = {"k": kd(ks), "q": [qda(qs), qdb(qs)], "E": (16384, 17408), "P": (18432, 19456)}

                def mask_fn(tabsel, q0, kc, nq):
                    m0 = q0 - 128 * kc + 1920
                    return Dtab[:, m0:m0 + nq].unsqueeze(1).broadcast_to([128, 2, nq])
                vl = lambda kc, vs=vs: vd(vs)[:, kc * 128:(kc + 1) * 128]
                ol = lambda kc: ones1
                pv = [[(0, vl, []), (1, ol, ["cmat"])], [(2, vl, []), (3, ol, ["cmat"])]]

                def fin(q0, nq, banks, fb, h=h):
                    tbq = q0 // 512
                    tz1, tz2, to1, to2, tr = tmp(), tmp(), tmp(), tmp(), tmp()

                    def s0(fb_):
                        for t_, b_ in ((tz1, banks[1]), (tz2, banks[3])):
                            act(TMP[:, t_, :], PS[b_][:, :], AF.Ln, ["ps%d" % b_], ["tmp%d" % t_])
                        for t_, b_ in ((to1, banks[0]), (to2, banks[2])):
                            S.add("dve", lambda e, t_=t_, b_=b_: e.tensor_copy(out=TMP[:, t_, :], in_=PS[b_][:, :]),
                                  reads=["ps%d" % b_], writes=["tmp%d" % t_])

                    def s1(fb_):
                        act(TMP[:, tz1, :], TMP[:, tz1, :], AF.Exp, ["tmp%d" % tz1], ["tmp%d" % tz1], scale=-1.0)

                    def s2(fb_):
                        act(TMP[:, tz2, :], TMP[:, tz2, :], AF.Exp, ["tmp%d" % tz2], ["tmp%d" % tz2], scale=-1.0)

                    def s3(fb_):
                        tt("dve", TMP[:, to1, :], TMP[:, to1, :], TMP[:, tz1, :], ALU.mult, ["tmp%d" % to1, "tmp%d" % tz1], ["tmp%d" % to1])
                        tt("dve", TMP[:, to2, :], TMP[:, to2, :], TMP[:, tz2, :], ALU.mult, ["tmp%d" % to2, "tmp%d" % tz2], ["tmp%d" % to2])

                    sqq = [0]

                    def s4(fb_):
                        stt(TMP[:, to1, :], TMP[:, to2, :], neglam, TMP[:, to1, :], ALU.mult, ALU.add,
                            ["tmp%d" % to1, "tmp%d" % to2, "lam2"], ["tmp%d" % to1])
                        sqq[0] = sqslot()
                        act(SQ[:, sqq[0], :], TMP[:, to1, :], AF.Square, ["tmp%d" % to1], ["sq%d" % sqq[0]])

                    def s5(fb_):
                        mm(fb_, 512, ones128, SQ[:, sqq[0], :], True, True, ["sq%d" % sqq[0], "cmat"])
                        act(TMP[:, tr, :], PS[fb_][:, :], AF.Ln, ["ps%d" % fb_, "vec"], ["tmp%d" % tr], bias=eps_ap, scale=1.0)
                        act(TMP[:, tr, :], TMP[:, tr, :], AF.Exp, ["tmp%d" % tr], ["tmp%d" % tr], scale=-0.5)
                        return True

                    def s6(fb_):
                        stt(AH[:, h, q0:q0 + 512], TMP[:, to1, :], vec[:, V_SUB8:V_SUB8 + 1], TMP[:, tr, :], ALU.mult, ALU.mult,
                            ["tmp%d" % to1, "tmp%d" % tr, "vec"], ["ah%d_%d" % (h, tbq)])
                    return [[s0], [s1], [s2], [s3], [s4], [s5], [s6]]
                attn_core(qs, ks, "v%d" % vs, FULLB, mask_fn, pv, fin, [[4, 5, 6, 7]],
                          skip_fn=(None if os.environ.get("NOSKIP") else (lambda q0, kc, nq, h=h: bool(ALIBI_ZERO[h][q0 // 512][kc]))), lay=lay)
            out_proj(C_AB_OUT, 0)
            NAB = [(0, 256, [0, 1, 2, 3], 0), (256, 256, [0, 1, 2, 3, 4, 5], 1), (512, 512, list(range(2, 10)), 1),
                   (1024, 512, list(range(6, 14)), 1), (1536, 256, list(range(10, 16)), 1), (1792, 256, [12, 13, 14, 15], 0)]
            S.fence(["q%d_%d" % (i, t) for i in range(2) for t in range(4)] + ["k%d_%d" % (i, t) for i in range(2) for t in range(4)] +
                    ["v%d_%d" % (i, t) for i in range(2) for t in range(4)] + ["ee0", "ee1", "ee2", "pp0", "pp1", "pp2"])
            for i in range(2):
                S.add("pool", lambda e, i=i: e.memset(vslot(i)[:, :, 64:128], 1.0), reads=[], writes=["v%d_%d" % (i, t4) for t4 in range(4)])
            for j in range(4):
                qs = ks = vs = j % 2
                proj_qk_norm(C_AB_IN + 12 + j, V_GQ8 + 1, qslot(qs), "q%d" % qs)
                proj_qk_norm(C_AB_IN + 16 + j, V_BK, kslot(ks), "k%d" % ks)
                slot = load_w(C_AB_IN + 20 + j)
                for t0 in range(0, 16, 4):
                    b = psbank()
                    for i in range(4):
                        tt_ = t0 + i
                        for kc in range(8):
                            mm(b, 128, xn[:, kc, tt_ * 128:(tt_ + 1) * 128], AW[:, slot, kc * 128:(kc + 1) * 128], kc == 0, kc == 7,
                               ["aw%d" % slot, nk(kc, tt_ // 4)], col0=i * 128)
                    src = PS[b][:, :].rearrange("p (t h c) -> p t h c", t=4, h=2)
                    V = vslot(vs)
                    S.add("dve", lambda e, V=V, src=src, t0=t0: e.tensor_copy(out=V[:, t0:t0 + 4, 0:64], in_=src[:, :, 0, :]),
                          reads=["ps%d" % b], writes=["v%d_%d" % (vs, t0 // 4)])
                    act(V[:, t0:t0 + 4, 128:192], src[:, :, 1, :], AF.Copy, ["ps%d" % b, "v%d_%d" % (vs, t0 // 4)], ["v%d_%d" % (vs, t0 // 4)])
                for hh in range(2):
                    dma("sp", NAtab[:, hh * 2 * NA_TW:(hh + 1) * 2 * NA_TW], natab[(2 * j + hh) * 128:(2 * j + hh + 1) * 128, :],
                        ["natab", "mask"], ["mask"], "natb")

                def mask_fn(tabsel, q0, kc, nq):
                    mp0 = q0 // 64 - 2 * kc + 7
                    c0 = tabsel * NA_TW + (mp0 + 3) * 64
                    return NAtab.rearrange("p (s c) -> p s c", s=2)[:, :, c0:c0 + nq]
                V = vslot(vs)
                pv = [[(0, lambda kc, V=V: V[:, kc, 0:128], [])], [(1, lambda kc, V=V: V[:, kc, 64:192], [])]]
                attn_core(qs, ks, "v%d" % vs, NAB, mask_fn, pv, fin_pair1(j, act_recip=True), [[6, 7]], nsp=3)
            out_proj(C_AB_OUT, 1)

        def phase_mix1():
            S.fence(["R", "gu0", "gu1", "dw0", "dw1", "mask"] + ["act%d_%d_%d" % (s_, j, tb) for s_ in range(2) for j in range(2) for tb in range(4)])
            dma("sp", TBL[:, 0:4096], rope_d, ["mask"], ["rope"], "rope")
            for i in range(2):
                S.add("pool", lambda e, i=i: e.memset(vslot(i)[:, :, 0:64], 1.0), reads=[], writes=["v%d_%d" % (i, t4) for t4 in range(4)])
                S.add("pool", lambda e, i=i: e.memset(vslot(i)[:, :, 128:192], 1.0), reads=[], writes=["v%d_%d" % (i, t4) for t4 in range(4)])

            def proj_rope(chunk, chunk_sw, gcol, gcol_sw, dst, dkey):
                s1 = load_w(chunk)
                s2 = load_w(chunk_sw)
                for tb in range(4):
                    cols = slice(tb * 512, (tb + 1) * 512)
                    b = proj_fm(s1, tb)
                    bs = proj_fm(s2, tb)
                    t = headnorm_rstd(b)
                    t1 = tmp()
                    stt(TMP[:, t1, :], PS[b][:, :], vec[:, gcol:gcol + 1], ropeC[:, cols], ALU.mult, ALU.mult,
                        ["ps%d" % b, "vec", "rope"], ["tmp%d" % t1])
                    t2 = tmp()
                    stt(TMP[:, t2, :], PS[bs][:, :], vec[:, gcol_sw:gcol_sw + 1], ropeS[:, cols], ALU.mult, ALU.mult,
                        ["ps%d" % bs, "vec", "rope"], ["tmp%d" % t2])
                    tt(os.environ.get("ADDENG", "dve"), TMP[:, t1, :], TMP[:, t1, :], TMP[:, t2, :], ALU.add, ["tmp%d" % t1, "tmp%d" % t2], ["tmp%d" % t1])
                    if isinstance(dst, tuple):
                        tt("dve", dst[0][0:64, cols], TMP[0:64, t1, :], TMP[0:64, t, :], ALU.mult, ["tmp%d" % t1, "tmp%d" % t], ["%s_%d" % (dkey, tb)])
                        tt("dve", dst[1][64:128, cols], TMP[64:128, t1, :], TMP[64:128, t, :], ALU.mult, ["tmp%d" % t1, "tmp%d" % t], ["%s_%d" % (dkey, tb)])
                    else:
                        tt("dve", dst[:, cols], TMP[:, t1, :], TMP[:, t, :], ALU.mult, ["tmp%d" % t1, "tmp%d" % t], ["%s_%d" % (dkey, tb)])
            qpa = (R[:, 0:2048], R[:, 14336:16384])
            qpb = (R[:, 2048:4096], R[:, 18432:20480])
            for i in range(2):
                S.add("pool", lambda e, i=i: e.memset(qpa[i][64:128, :], 0.0), reads=[], writes=["q%d_%d" % (i, t4) for t4 in range(4)])
                S.add("pool", lambda e, i=i: e.memset(qpb[i][0:64, :], 0.0), reads=[], writes=["q%d_%d" % (i, t4) for t4 in range(4)])
            CUT = int(os.environ.get("MIX1_CUT", "99"))
            if CUT <= 1:
                return
            for j in range(8):
                kv = j // 2
                qs = j % 2
                ks = vs = kv % 2
                if j % 2 == 0:
                    proj_rope(C_CK + kv, C_CKS + kv, V_CK, V_CKS, kslot(ks), "k%d" % ks)
                    if CUT <= 2:
                        return
                    V = vslot(vs)
                    proj_v_tm(C_CV + kv // 2, (kv % 2) * 64, 64, lambda tt_, V=V: V[:, tt_, 64:128], "v%d" % vs)
                    if CUT <= 3:
                        return
                VAR = os.environ.get("VARQ", "")
                if VAR == "A":
                    proj_rope(C_CK + kv, C_CKS + kv, V_CK, V_CKS, kslot(ks), "k%d" % ks)
                elif VAR == "B":
                    proj_rope(C_CQ + j, C_CQS + j, V_CK, V_CKS, qslot(qs), "q%d" % qs)
                elif VAR == "C":
                    proj_rope(C_CQ + j, C_CQS + j, V_GQ8 + 2, V_GQ8 + 3, kslot(ks), "k%d" % ks)
                else:
                    proj_rope(C_CQ + j, C_CQS + j, V_GQ8 + 2, V_GQ8 + 3, (qpa[qs], qpb[qs]), "q%d" % qs)
                if CUT <= 4:
                    return
                V = vslot(vs)
                pv = [[(0, lambda kc, V=V: V[:, kc, 64:192], [])], [(1, lambda kc, V=V: V[:, kc, 0:128], [])]]
                lay = {"k": kslot(ks), "q": [qpa[qs], qpb[qs]], "E": (), "P": (16384, 17408)}
                attn_core(qs, ks, "v%d" % vs, FULLB, None, pv, fin_pair1(j % 4), [[6, 7]], nsp=3, lay=lay)
                if j % 4 == 3:
                    out_proj(C_C_OUT, j // 4)

        def ffn_fence():
            S.fence(["R", "mask", "rope", "stg0", "stg1", "stgi0", "stgi1"] + ["q%d_%d" % (i, t) for i in range(2) for t in range(4)] +
                    ["k%d_%d" % (i, t) for i in range(2) for t in range(4)] + ["v%d_%d" % (i, t) for i in range(2) for t in range(4)] +
                    ["ee0", "ee1", "ee2", "pp0", "pp1", "pp2"])

        ACTK = ["act%d_%d_%d" % (s_, j, tb) for s_ in range(2) for j in range(2) for tb in range(4)]
        for s in range(nseq):
            if s == 0:
                ffn_fence()
                phase_in(s)
            if stop < 99:
                ffn_fence()
                for l in range(2):
                    if stop <= 3 * l:
                        break
                    phase_norm(V_FFN1 + 8 * l, False)
                    phase_ffn(l * 2 + 0)
                    if stop <= 3 * l + 1:
                        break
                    phase_norm(V_MIX + 8 * l, False)
                    if l == 0:
                        phase_mix0()
                    else:
                        phase_mix1()
                    if stop <= 3 * l + 2:
                        break
                    ffn_fence()
                    phase_norm(V_FFN2 + 8 * l, False)
                    phase_ffn(l * 2 + 1)
                    phase_norm(V_FIN + 8 * l, True)
            else:
                phase_norm(V_FFN1, False)
                ffn_fence()
                for l in range(2):
                    if s == 0 and l == 0:
                        cast_ffn(1)
                    phase_ffn(l * 2 + 0, chain=norm_chain([(V_MIX + 8 * l, False)]))
                    if s == 0 and l == 0:
                        cast_ffn(2)
                    if l == 0:
                        phase_mix0()
                    else:
                        phase_mix1()
                    phase_norm(V_FFN2 + 8 * l, False)
                    ffn_fence()
                    if s == 0 and l == 0:
                        cast_ffn(3)
                    specs = [(V_FIN + 8 * l, True)] + ([(V_FFN1 + 8, False)] if l == 0 else [])
                    phase_ffn(l * 2 + 1, chain=norm_chain(specs))
            ffn_fence()
            S.fence(["gu0", "gu1", "dw0", "dw1"] + ACTK)
            if s + 1 < nseq and stop >= 99:
                phase_out_in(s)
            else:
                phase_out(s)
                if s + 1 < nseq:
                    phase_in(s + 1)
        S.add("sp", None, reads=["yout0", "yout1"])
        S.emit(st)
    return nc


def _chunk(w, cols):
    blk = w[:, cols]
    return np.ascontiguousarray(blk.reshape(8, 128, 128).transpose(1, 0, 2).reshape(128, 1024))


def _consts():
    identf = np.eye(128, dtype=np.float32)
    cm = np.zeros((128, 512), np.float32)
    cm[:, 0:128] = 1.0 / 1024
    cm[0:64, 128:192] = 1.0 / 64
    cm[64:128, 192:256] = 1.0 / 64
    cm[:, 256:384] = 1.0 / 128
    cm[:, 384:512] = 1.0
    cmat = cm.astype(ml_dtypes.bfloat16)
    p = np.arange(128)[:, None]
    m = np.arange(DW)[None, :]
    dist = np.abs(m - p - 1920).astype(np.float64)
    slopes = 2.0 ** (-8.0 * np.arange(1, 5) / 4)
    alibi = np.concatenate([np.exp(-sl * dist) for sl in slopes], axis=0).astype(np.float32).astype(ml_dtypes.bfloat16)
    f = np.arange(128) % 64
    mm_ = f % 16
    inv_freq = (10000.0 ** (-(np.arange(0, 32, 2, dtype=np.float32)) / 32)).astype(np.float32)
    t = np.arange(S_LEN)
    pos = np.where((f[:, None] // 32) == 0, (t[None, :] // 64), (t[None, :] % 64)).astype(np.float32)
    ang = pos * inv_freq[mm_][:, None]
    sign = np.where((f % 32) < 16, -1.0, 1.0)[:, None]
    rope = np.concatenate([np.cos(ang), sign * np.sin(ang)], axis=1).astype(np.float32)
    pp = np.arange(128)
    half = pp // 64
    ck = pp % 64
    mi = np.arange(NA_NM)
    cq = np.arange(64)
    dr = half[:, None, None] + 7 - (mi[None, :, None] - 3)
    dc = ck[:, None, None] - cq[None, None, :]
    cstart = np.clip(cq - 8, 0, 48)[None, None, :]
    colv = (ck[:, None, None] >= cstart) & (ck[:, None, None] < cstart + 16)
    vF = (np.abs(dr) <= 7) & colv
    vI = (dr >= -4) & (dr <= 3) & colv
    navalid = np.concatenate([vF.reshape(128, -1), vI.reshape(128, -1)], axis=1).astype(np.float32).astype(ml_dtypes.bfloat16)
    dr_b = np.broadcast_to(dr, vF.shape)
    dc_b = np.broadcast_to(dc, vF.shape)
    return identf, cmat, alibi, rope, navalid, vF, dr_b, dc_b


def _prep(inp):
    g = lambda k: np.asarray(inp[k], dtype=np.float32)
    blocks = []
    for l in range(2):
        for nm in ("ffn1", "ffn2"):
            wg = g(nm + "_w_gate")[l].reshape(8, 128, NG, 256).transpose(2, 1, 0, 3).reshape(NG, 128, 2048)
            wu = g(nm + "_w_up")[l].reshape(8, 128, NG, 256).transpose(2, 1, 0, 3).reshape(NG, 128, 2048)
            wd = g(nm + "_w_down")[l].reshape(NG, 2, 128, 1024).transpose(0, 2, 1, 3).reshape(NG, 128, 2048)
            blocks.append(np.concatenate([wg, wu, wd], axis=2))
    wffn = np.ascontiguousarray(np.stack(blocks).reshape(4 * NG * 128, 6144))
    ch = []
    abin = g("ab_w_in")[0]
    for c in range(24):
        ch.append(_chunk(abin, slice(c * 128, (c + 1) * 128)))
    abo = g("ab_w_out")[0]
    for c in range(8):
        ch.append(_chunk(abo, slice(c * 128, (c + 1) * 128)))
    cin = g("c_w_in")[0]
    sw = np.arange(1024) ^ 16
    wq = cin[:, 0:1024]
    for c in range(8):
        ch.append(_chunk(wq, slice(c * 128, (c + 1) * 128)))
    wqs = wq[:, sw]
    for c in range(8):
        ch.append(_chunk(wqs, slice(c * 128, (c + 1) * 128)))
    wk = cin[:, 1024:1280]
    wks = wk[:, np.arange(256) ^ 16]
    for src in (wk, wks):
        for kv in range(4):
            cols = np.concatenate([np.arange(kv * 64, kv * 64 + 64)] * 2)
            ch.append(_chunk(src, cols))
    wv = cin[:, 1280:1536]
    for c in range(2):
        ch.append(_chunk(wv, slice(c * 128, (c + 1) * 128)))
    co = g("c_w_out")[0]
    for c in range(8):
        ch.append(_chunk(co, slice(c * 128, (c + 1) * 128)))
    watt = np.ascontiguousarray(np.concatenate(ch, axis=0))
    assert watt.shape == (N_ATT * 128, 1024)
    vec = np.zeros((128, NVEC), np.float32)
    for l in range(2):
        for base, nm in ((V_FFN1, "ffn1_norm"), (V_MIX, "mix_norm"), (V_FFN2, "ffn2_norm"), (V_FIN, "final_norm")):
            vec[:, base + 8 * l:base + 8 * l + 8] = g(nm)[l].reshape(8, 128).T
    t2 = lambda v: np.concatenate([v, v])
    vec[:, V_AQ] = t2(g("a_q_norm")[0]); vec[:, V_AK] = t2(g("a_k_norm")[0])
    vec[:, V_BQ] = t2(g("b_q_norm")[0]); vec[:, V_BK] = t2(g("b_k_norm")[0])
    vec[:, V_SUB] = g("a_sub_norm")[0]
    s64 = np.arange(64) ^ 16
    vec[:, V_CQ] = t2(g("c_q_norm")[0]); vec[:, V_CQS] = t2(g("c_q_norm")[0][s64])
    vec[:, V_CK] = t2(g("c_k_norm")[0]); vec[:, V_CKS] = t2(g("c_k_norm")[0][s64])
    lam = np.concatenate([g("a_lambda_q1")[0], g("a_lambda_k1")[0], g("a_lambda_q2")[0], g("a_lambda_k2")[0]])
    lamv = np.ascontiguousarray(np.broadcast_to(lam[None, :], (128, 256)))
    identf, cmat, alibi, rope, navalid, vF, dr_b, dc_b = _consts()
    rpb = g("b_rpb")[0]
    rpbt = np.zeros((8,) + vF.shape, np.float32)
    idx = np.nonzero(vF)
    for h in range(8):
        rpbt[h][idx] = rpb[h][dr_b[idx] + 7, dc_b[idx] + 15]
    rpbt = np.ascontiguousarray(rpbt.reshape(8 * 128, NA_TW))
    return dict(wffn=wffn, watt=watt, vec=vec, lamv=lamv, identf=identf, cmat=cmat, alibi=alibi, rope=rope,
                rpbt=rpbt, navalid=navalid)


def kernel(**inputs):
    xp = np.asarray(inputs["x_prompt"], dtype=np.float32)
    xs = np.asarray(inputs["x_sample"], dtype=np.float32)
    shared = _prep(inputs)
    npc, nsc = xp.shape[0] // NCORE, xs.shape[0] // NCORE
    nseq = npc + nsc
    nc = build(nseq)
    in_maps = []
    for i in range(NCORE):
        xi = np.concatenate([xp[i * npc:(i + 1) * npc], xs[i * nsc:(i + 1) * nsc]], axis=0).reshape(nseq * S_LEN, D)
        m = dict(shared)
        m["xin"] = np.ascontiguousarray(xi)
        in_maps.append(m)
    res = run_bass_kernel_spmd(nc, in_maps, core_ids=list(range(NCORE)))
    yp = np.empty_like(xp)
    ys = np.empty_like(xs)
    for i in range(NCORE):
        y = np.asarray(res.results[i]["yout"]).reshape(nseq, S_LEN, D)
        yp[i * npc:(i + 1) * npc] = y[:npc]
        ys[i * nsc:(i + 1) * nsc] = y[npc:]
    return (yp, ys)
```

```python
import math
import os
import numpy as np
from contextlib import ExitStack
import ml_dtypes
import concourse.bass as bass
import concourse.mybir as mybir
from concourse.bass_utils import run_bass_kernel_spmd

F32 = mybir.dt.float32
BF16 = mybir.dt.bfloat16
ALU = mybir.AluOpType
AF = mybir.ActivationFunctionType

D = 1024
S_LEN = 2048
DFF = 2816
NG = 11
EPS = 1e-6
NCORE = 8
NA_NM = 22
NA_TW = NA_NM * 64
DW = 3968
LAMBDA_INIT0 = 0.8 - 0.6 * math.exp(-0.3 * 0)

C_AB_IN = 0
C_AB_OUT = 24
C_CQ = 32
C_CQS = 40
C_CK = 48
C_CKS = 52
C_CV = 56
C_C_OUT = 58
N_ATT = 66

V_FFN1 = 0
V_MIX = 16
V_FFN2 = 32
V_FIN = 48
V_AQ = 64
V_AK = 65
V_BQ = 66
V_BK = 67
V_SUB = 68
V_CQ = 69
V_CQS = 70
V_CK = 71
V_CKS = 72
NVEC = 73


class Op:
    __slots__ = ("eng", "fn", "idx", "signal", "waits", "dma_key", "dma_cnt", "vc", "sigcnt", "dwaits")


class Sched:
    ENGS = ("pe", "act", "dve", "pool", "sp")
    EPOCH = 30000

    def __init__(self, nc):
        self.nc = nc
        self.ops = {e: [] for e in self.ENGS}
        self.last_w = {}
        self.real_w = {}
        self.readers = {}
        self.known = {e: {} for e in self.ENGS}
        self.known_dma = {e: {} for e in self.ENGS}
        self.dma_cnt = {}

    def add(self, eng, fn, reads=(), writes=(), dma=None, record=True):
        op = Op()
        op.eng = eng; op.fn = fn; op.signal = False; op.waits = []; op.dwaits = []
        op.dma_key = dma; op.dma_cnt = 0
        lst = self.ops[eng]
        op.idx = len(lst) + 1
        deps = []
        raw = set()
        ps_reads = [r for r in reads if r.startswith("ps")] if fn is not None else []
        true_writes = writes
        if ps_reads:
            writes = list(writes) + ps_reads
        for r in reads:
            w = self.last_w.get(r)
            if w is not None:
                deps.append(w)
                if self.real_w.get(r) is w:
                    raw.add(id(w))
        for w_ in writes:
            lw = self.last_w.get(w_)
            if lw is not None:
                deps.append(lw)
            rl = self.readers.get(w_)
            if rl:
                deps.extend(rl)
        known = self.known[eng]
        kdma = self.known_dma[eng]
        for y in deps:
            if y is op:
                continue
            if y.dma_key is not None:
                if kdma.get(y.dma_key, 0) >= y.dma_cnt:
                    continue
                kdma[y.dma_key] = y.dma_cnt
                op.dwaits.append((y.dma_key, y.dma_cnt))
                for k, v in y.vc.items():
                    if known.get(k, 0) < v:
                        known[k] = v
                continue
            if y.eng == eng:
                if eng == "pe" or eng == "sp":
                    continue
                if id(y) not in raw:
                    continue
            if known.get(y.eng, 0) >= y.idx:
                continue
            y.signal = True
            op.waits.append(y)
            for k, v in y.vc.items():
                if known.get(k, 0) < v:
                    known[k] = v
            known[y.eng] = y.idx
        op.vc = dict(known)
        if dma is not None:
            c = self.dma_cnt.get(dma, 0) + 16
            self.dma_cnt[dma] = c
            op.dma_cnt = c
        if record and fn is not None:
            for r in reads:
                self.readers.setdefault(r, []).append(op)
            for w_ in writes:
                self.last_w[w_] = op
                self.readers[w_] = []
            for w_ in true_writes:
                self.real_w[w_] = op
        lst.append(op)
        return op

    def fence(self, keys):
        keys = list(keys)
        for e in self.ENGS:
            self.add(e, None, reads=keys, writes=keys, record=False)

    def emit(self, stack):
        nc = self.nc
        sems = {}

        def sem(name):
            if name not in sems:
                sems[name] = stack.enter_context(nc.semaphore("s_" + name.replace(":", "_")))
            return sems[name]
        for e in self.ENGS:
            c = 0
            for op in self.ops[e]:
                if op.signal:
                    c += 1
                    op.sigcnt = c
        for e in self.ENGS:
            for op in self.ops[e]:
                if op.signal:
                    sem("%s:%d" % (e, (op.sigcnt - 1) // self.EPOCH))
                if op.dma_key is not None:
                    sem("d:" + str(op.dma_key))
        block = stack.enter_context(nc.Block())
        EP = self.EPOCH

        def run(engobj, e):
            for op in self.ops[e]:
                for y in op.waits:
                    ep = (y.sigcnt - 1) // EP
                    engobj.wait_ge(sem("%s:%d" % (y.eng, ep)), y.sigcnt - ep * EP)
                for key, cnt in op.dwaits:
                    engobj.wait_ge(sem("d:" + str(key)), cnt)
                if op.fn is None:
                    continue
                ins = op.fn(engobj)
                if op.dma_key is not None:
                    ins.then_inc(sem("d:" + str(op.dma_key)), 16)
                elif op.signal:
                    ep = (op.sigcnt - 1) // EP
                    ins.then_inc(sem("%s:%d" % (e, ep)), 1)

        @block.tensor
        def _(eng): run(eng, "pe")

        @block.scalar
        def _(eng): run(eng, "act")

        @block.vector
        def _(eng): run(eng, "dve")

        @block.gpsimd
        def _(eng): run(eng, "pool")

        @block.sync
        def _(eng): run(eng, "sp")


_ALIBI_ZERO = []


def _alibi_zero():
    if not _ALIBI_ZERO:
        al = np.asarray(_consts()[2]).astype(np.float32).reshape(4, 128, DW)
        z = np.zeros((4, 4, 16), bool)
        for h in range(4):
            for qb in range(4):
                for kc in range(16):
                    m0 = qb * 512 - 128 * kc + 1920
                    z[h, qb, kc] = not np.any(al[h][:, m0:m0 + 512])
        _ALIBI_ZERO.append(z)
    return _ALIBI_ZERO[0]


def build(nseq, stop=99):
    ALIBI_ZERO = _alibi_zero()
    nc = bass.Bass("TRN2", target_bir_lowering=False)
    dt = lambda n, s, d, k: nc.dram_tensor(n, s, d, kind=k).ap()
    xin = dt("xin", [nseq * S_LEN, D], F32, "ExternalInput")
    yout = dt("yout", [nseq * S_LEN, D], F32, "ExternalOutput")
    wffn32 = dt("wffn", [4 * NG * 128, 6144], F32, "ExternalInput")
    watt32 = dt("watt", [N_ATT * 128, 1024], F32, "ExternalInput")
    vec_d = dt("vec", [128, NVEC], F32, "ExternalInput")
    lamv_d = dt("lamv", [128, 256], F32, "ExternalInput")
    identf_d = dt("identf", [128, 128], F32, "ExternalInput")
    cmat_d = dt("cmat", [128, 512], BF16, "ExternalInput")
    alibi_d = dt("alibi", [4 * 128, DW], BF16, "ExternalInput")
    rope_d = dt("rope", [128, 4096], F32, "ExternalInput")
    rpbt_d = dt("rpbt", [8 * 128, NA_TW], F32, "ExternalInput")
    navalid_d = dt("navalid", [128, 2 * NA_TW], BF16, "ExternalInput")
    wffn = dt("wffn_b", [4 * NG * 128, 6144], BF16, "Internal")
    watt = dt("watt_b", [N_ATT * 128, 1024], BF16, "Internal")
    natab = dt("natab", [8 * 128, 2 * NA_TW], BF16, "Internal")

    st = ExitStack()
    with st:
        sb = lambda n, s, d: st.enter_context(nc.sbuf_tensor(n, s, d))
        xT = sb("xT", [128, 8, S_LEN], F32)
        xn = sb("xn", [128, 8, S_LEN], BF16)
        R = sb("R", [128, 20480], BF16)
        TMP = sb("TMP", [128, 8, 512], F32)
        SQ = sb("SQ", [128, 4, 512], BF16)
        AH = sb("AH", [128, 4, S_LEN], BF16)
        AW = sb("AW", [128, 6, 1024], BF16)
        TBL = sb("TBL", [128, 5120], F32)
        identf = sb("identf_s", [128, 128], F32)
        cmat = sb("cmat_s", [128, 512], BF16)
        vec = sb("vec_s", [128, NVEC + 8], F32)
        lamt = sb("lamt", [128, 256], F32)
        lam2 = sb("lam2", [128, 8], F32)
        PSA = st.enter_context(nc.psum_tensor("psa", [128, 4096], F32))
        PS = [PSA[:, i * 512:(i + 1) * 512] for i in range(8)]
        S = Sched(nc)

        ones1024 = cmat[:, 0:128]
        blk64 = cmat[:, 128:256]
        ones128 = cmat[:, 256:384]
        ones1 = cmat[:, 384:512]
        eps_ap = vec[:, NVEC:NVEC + 1]
        V_GQ8 = NVEC + 1
        neglam = lam2[:, 4:5]
        V_SUB8 = NVEC + 5

        TBLb = TBL[:].bitcast(BF16)
        Dtab = TBLb[:, 0:DW]
        NAtab = TBLb[:, 4096:4096 + 4 * NA_TW]
        ropeC = TBL[:, 0:2048]
        ropeS = TBL[:, 2048:4096]
        Rf = R[:].bitcast(F32)

        psi = [0]

        def psbank(lo=0, hi=8):
            b = lo + psi[0] % (hi - lo)
            psi[0] += 1
            return b

        def mm(b, ncol, lhsT, rhs, start, stop, reads, col0=0, prow=None):
            out = PS[b][:, col0:col0 + ncol] if prow is None else PS[b][prow[0]:prow[1], col0:col0 + ncol]
            S.add("pe", lambda e, o=out, l=lhsT, r=rhs, s=start, p=stop: e.matmul(o, l, r, start=s, stop=p),
                  reads=reads, writes=["ps%d" % b])

        def act(out, in_, func, reads, writes, bias=None, scale=None):
            kw = {}
            if bias is not None:
                kw["bias"] = bias
            if scale is not None:
                kw["scale"] = scale
            S.add("act", lambda e, o=out, i=in_, f=func, kw=kw: e.activation(out=o, in_=i, func=f, **kw),
                  reads=reads, writes=writes)

        def tt(eng, out, in0, in1, op, reads, writes):
            S.add(eng, lambda e, o=out, a=in0, b=in1, p=op: e.tensor_tensor(out=o, in0=a, in1=b, op=p),
                  reads=reads, writes=writes)

        def stt(out, in0, scalar, in1, op0, op1, reads, writes):
            S.add("dve", lambda e, o=out, a=in0, s=scalar, b=in1, p0=op0, p1=op1:
                  e.scalar_tensor_tensor(out=o, in0=a, scalar=s, in1=b, op0=p0, op1=p1),
                  reads=reads, writes=writes)

        def dma(q, out, in_, reads, writes, key):
            S.add(q, lambda e, o=out, i=in_: e.dma_start(out=o, in_=i), reads=reads, writes=writes, dma=key)

        tmpi = [0]

        def tmp():
            i = tmpi[0] % 8
            tmpi[0] += 1
            return i

        sqi = [0]

        def sqslot():
            i = sqi[0] % 4
            sqi[0] += 1
            return i

        def rstd_from_psum(b, ncol, extra_reads=()):
            t = tmp()
            act(TMP[:, t, 0:ncol], PS[b][:, 0:ncol], AF.Ln, ["ps%d" % b, "vec"], ["tmp%d" % t], bias=eps_ap, scale=1.0)
            act(TMP[:, t, 0:ncol], TMP[:, t, 0:ncol], AF.Exp, ["tmp%d" % t], ["tmp%d" % t], scale=-0.5)
            return t

        dma("sp", identf[:], identf_d, [], ["identf"], "c_id")
        dma("sp", cmat[:], cmat_d, [], ["cmat"], "c_cm")
        dma("sp", vec[:, 0:NVEC], vec_d, [], ["vec"], "c_vec")
        dma("sp", lamt[:], lamv_d, [], ["lamt"], "c_lam")
        S.add("pool", lambda e: e.memset(vec[:, NVEC:NVEC + 1], EPS), reads=["vec"], writes=["vec"])
        for i, c in enumerate((V_AQ, V_BQ, V_CQ, V_CQS)):
            S.add("dve", lambda e, i=i, c=c: e.tensor_scalar(out=vec[:, V_GQ8 + i:V_GQ8 + i + 1], in0=vec[:, c:c + 1],
                                                            scalar1=0.125, scalar2=None, op0=ALU.mult),
                  reads=["vec"], writes=["vec"])
        S.add("dve", lambda e: e.tensor_scalar(out=vec[:, V_SUB8:V_SUB8 + 1], in0=vec[:, V_SUB:V_SUB + 1],
                                               scalar1=1.0 - LAMBDA_INIT0, scalar2=None, op0=ALU.mult),
              reads=["vec"], writes=["vec"])
        S.add("dve", lambda e: e.tensor_tensor(out=lamt[:, 0:64], in0=lamt[:, 0:64], in1=lamt[:, 64:128], op=ALU.mult),
              reads=["lamt"], writes=["lamt"])
        S.add("dve", lambda e: e.tensor_tensor(out=lamt[:, 128:192], in0=lamt[:, 128:192], in1=lamt[:, 192:256], op=ALU.mult),
              reads=["lamt"], writes=["lamt"])
        S.add("dve", lambda e: e.reduce_sum(out=lam2[:, 0:1], in_=lamt[:, 0:64], axis=mybir.AxisListType.X),
              reads=["lamt"], writes=["lam2"])
        S.add("dve", lambda e: e.reduce_sum(out=lam2[:, 1:2], in_=lamt[:, 128:192], axis=mybir.AxisListType.X),
              reads=["lamt", "lam2"], writes=["lam2"])
        act(lam2[:, 2:4], lam2[:, 0:2], AF.Exp, ["lam2"], ["lam2"])
        S.add("dve", lambda e: e.scalar_tensor_tensor(out=lam2[:, 4:5], in0=lam2[:, 3:4], scalar=-LAMBDA_INIT0, in1=lam2[:, 2:3],
                                                      op0=ALU.add, op1=ALU.subtract),
              reads=["lam2"], writes=["lam2"])
        def cast_ffn(fi):
            for i in range(fi * NG, (fi + 1) * NG):
                key = "wffn0_g%d" % (i - fi * NG) if fi == 0 else "wffn%d" % fi
                sem_ = "cw0_%d" % (i - fi * NG) if fi == 0 else "cw%d" % fi
                dma("pool", wffn[i * 128:(i + 1) * 128, :], wffn32[i * 128:(i + 1) * 128, :], [], [key], sem_)
        cast_ffn(0)
        for i in range(0, N_ATT, 6):
            dma("pool", watt[i * 128:(i + 6) * 128, :], watt32[i * 128:(i + 6) * 128, :], [], ["watt"], "ca")
        nav = R[:, 0:2 * NA_TW]
        dma("sp", nav, navalid_d, [], ["nav"], "c_nav")
        for h in range(8):
            dma("sp", TBL[:, 0:NA_TW], rpbt_d[h * 128:(h + 1) * 128, :], ["tbl_e"], ["tbl_r"], "c_rp")
            act(TBL[:, 2048:2048 + NA_TW], TBL[:, 0:NA_TW], AF.Exp, ["tbl_r"], ["tbl_e"])
            ob = TBLb[:, 7168:7168 + 2 * NA_TW]
            for k in range(2):
                tt("dve", ob[:, k * NA_TW:(k + 1) * NA_TW], TBL[:, 2048:2048 + NA_TW], nav[:, k * NA_TW:(k + 1) * NA_TW],
                   ALU.mult, ["tbl_e", "nav", "tbl_o"], ["tbl_o"])
            dma("sp", natab[h * 128:(h + 1) * 128, :], ob, ["tbl_o"], ["natab"], "c_nt")
        S.fence(["nav", "tbl_r", "tbl_e", "tbl_o", "R", "TBL"])

        XK = ["xT%d_%d" % (c, tb) for c in range(8) for tb in range(4)]
        xk = lambda c, tb: "xT%d_%d" % (c, tb)
        nk = lambda c, tb: "xn%d_%d" % (c, tb)

        def in_tile(s, tt_, q="sp", base=0, pfx="stg", spfx="ld"):
            slot = tt_ % 2
            stg = Rf[:, base + slot * 1024:base + (slot + 1) * 1024]
            dma(q, stg, xin[(s * 16 + tt_) * 128:(s * 16 + tt_ + 1) * 128, :], [], ["%s%d" % (pfx, slot)], "%s%d" % (spfx, slot))
            for half in range(2):
                b = psbank()
                for c4 in range(4):
                    c = half * 4 + c4
                    S.add("pe", lambda e, b=b, c4=c4, c=c, stg=stg: e.transpose(PS[b][:, c4 * 128:(c4 + 1) * 128],
                                                                              stg[:, c * 128:(c + 1) * 128], identf[:]),
                          reads=["%s%d" % (pfx, slot), "identf"], writes=["ps%d" % b])
                tb = tt_ // 4
                o = xT[:, half * 4:half * 4 + 4, tt_ * 128:(tt_ + 1) * 128]
                i = PS[b][:, :].rearrange("p (c t) -> p c t", c=4)
                wr = [xk(half * 4 + c4, tb) for c4 in range(4)]
                if half == 0:
                    S.add("dve", lambda e, o=o, i=i: e.tensor_copy(out=o, in_=i), reads=["ps%d" % b], writes=wr)
                else:
                    act(o, i, AF.Copy, ["ps%d" % b], wr)

        def out_tile(s, tt_):
            slot = tt_ % 2
            stg = Rf[:, slot * 1024:(slot + 1) * 1024]
            tb = tt_ // 4
            for half in range(2):
                b = psbank()
                for c4 in range(4):
                    c = half * 4 + c4
                    S.add("pe", lambda e, b=b, c4=c4, c=c, tt_=tt_: e.transpose(PS[b][:, c4 * 128:(c4 + 1) * 128],
                                                                      xT[:, c, tt_ * 128:(tt_ + 1) * 128], identf[:]),
                          reads=[xk(c, tb), "identf"], writes=["ps%d" % b])
                o = stg[:, half * 512:(half + 1) * 512]
                if half == 0:
                    S.add("dve", lambda e, o=o, b=b: e.tensor_copy(out=o, in_=PS[b][:, :]), reads=["ps%d" % b],
                          writes=["stg%d" % slot])
                else:
                    act(o, PS[b][:, :], AF.Copy, ["ps%d" % b], ["stg%d" % slot])
            dma("sp", yout[(s * 16 + tt_) * 128:(s * 16 + tt_ + 1) * 128, :], stg, ["stg%d" % slot], ["yout%d" % slot], "st%d" % slot)

        def phase_in(s):
            for tt_ in range(16):
                in_tile(s, tt_)

        def phase_out(s):
            for tt_ in range(16):
                out_tile(s, tt_)

        def phase_out_in(s):
            for i_ in range(20):
                if i_ < 16:
                    out_tile(s, i_)
                if i_ >= 4:
                    in_tile(s + 1, i_ - 4, q="pool", base=2048, pfx="stgi", spfx="ldi")

        def norm_stats(tb):
            cols = slice(tb * 512, (tb + 1) * 512)
            b = psbank()
            for c in range(8):
                q = sqslot()
                if c % 2 == 0:
                    act(SQ[:, q, :], xT[:, c, cols], AF.Square, [xk(c, tb)], ["sq%d" % q])
                else:
                    tt("pool", SQ[:, q, :], xT[:, c, cols], xT[:, c, cols], ALU.mult, [xk(c, tb)], ["sq%d" % q])
                mm(b, 512, ones1024, SQ[:, q, :], c == 0, c == 7, ["sq%d" % q, "cmat"])
            return rstd_from_psum(b, 512)

        def norm_apply(gcol, inplace, tb, t):
            cols = slice(tb * 512, (tb + 1) * 512)
            for c in range(8):
                g = vec[:, gcol + c:gcol + c + 1]
                if inplace:
                    stt(xT[:, c, cols], xT[:, c, cols], g, TMP[:, t, :], ALU.mult, ALU.mult,
                        [xk(c, tb), "tmp%d" % t, "vec"], [xk(c, tb)])
                else:
                    stt(xn[:, c, cols], xT[:, c, cols], g, TMP[:, t, :], ALU.mult, ALU.mult,
                        [xk(c, tb), "tmp%d" % t, "vec"], [nk(c, tb)])

        def phase_norm(gcol, inplace):
            for tb in range(4):
                t = norm_stats(tb)
                norm_apply(gcol, inplace, tb, t)

        def norm_chain(specs):
            def make(tb):
                hold = [None]
                stages = []

                def st_first():
                    hold[0] = norm_stats(tb)
                stages.append(st_first)
                for i, (gcol, inplace) in enumerate(specs):
                    def st_(i=i, gcol=gcol, inplace=inplace):
                        norm_apply(gcol, inplace, tb, hold[0])
                        if i + 1 < len(specs):
                            hold[0] = norm_stats(tb)
                    stages.append(st_)
                return stages
            return make

        def phase_ffn(fi, chain=None, preloaded=False):
            gus = lambda slot: R[:, slot * 4096:(slot + 1) * 4096]
            dws = lambda slot: R[:, 8192 + slot * 2048:8192 + (slot + 1) * 2048]
            ab = lambda slot: R[:, 12288 + slot * 4096:12288 + (slot + 1) * 4096]

            wkey = lambda g: ("wffn0_g%d" % g) if fi == 0 else ("wffn%d" % fi)

            def load_gu(g):
                slot = g % 2
                rows = slice((fi * NG + g) * 128, (fi * NG + g + 1) * 128)
                dma("sp", gus(slot), wffn[rows, 0:4096], [wkey(g)], ["gu%d" % slot], "gu%d" % slot)

            def load_dw(g):
                slot = g % 2
                rows = slice((fi * NG + g) * 128, (fi * NG + g + 1) * 128)
                dma("sp", dws(slot), wffn[rows, 4096:6144], [wkey(g)], ["dw%d" % slot], "dw%d" % slot)

            def up_step(g, tb, j):
                slot = g % 2
                w = gus(slot)
                a = ab(slot)
                cols = slice(tb * 512, (tb + 1) * 512)
                bg = psbank()
                for kc in range(8):
                    mm(bg, 512, w[:, kc * 256 + j * 128:kc * 256 + (j + 1) * 128], xn[:, kc, cols], kc == 0, kc == 7,
                       ["gu%d" % slot, nk(kc, tb)])
                bu = psbank()
                for kc in range(8):
                    mm(bu, 512, w[:, 2048 + kc * 256 + j * 128:2048 + kc * 256 + (j + 1) * 128], xn[:, kc, cols],
                       kc == 0, kc == 7, ["gu%d" % slot, nk(kc, tb)])
                t = tmp()
                act(TMP[:, t, :], PS[bg][:, :], AF.Silu, ["ps%d" % bg], ["tmp%d" % t])
                tt("dve", a[:, j * 2048 + tb * 512:j * 2048 + (tb + 1) * 512], PS[bu][:, :], TMP[:, t, :], ALU.mult,
                   ["ps%d" % bu, "tmp%d" % t], ["act%d_%d_%d" % (slot, j, tb)])

            def down_step(g, tb, oc):
                slot = g % 2
                w = dws(slot)
                a = ab(slot)
                cols = slice(tb * 512, (tb + 1) * 512)
                b = psbank()
                for j in range(2):
                    mm(b, 512, w[:, j * 1024 + oc * 128:j * 1024 + (oc + 1) * 128],
                       a[:, j * 2048 + tb * 512:j * 2048 + (tb + 1) * 512], j == 0, j == 1,
                       ["dw%d" % slot, "act%d_%d_%d" % (slot, j, tb)])
                stt(xT[:, oc, cols], PS[b][:, :], 0.5, xT[:, oc, cols], ALU.mult, ALU.add,
                    ["ps%d" % b, xk(oc, tb)], [xk(oc, tb)])
            if fi == "prefetch":
                return load_gu, load_dw
            if not preloaded:
                load_gu(0)
                load_dw(0)
            pending = []
            for g in range(NG + 1):
                if g + 1 < NG:
                    load_gu(g + 1)
                for st_ in range(8):
                    if g < NG:
                        up_step(g, st_ // 2, st_ % 2)
                    if g >= 1:
                        for k in range(4):
                            down_step(g - 1, st_ // 2, (st_ % 2) * 4 + k)
                    if g == NG and chain is not None and st_ % 2 == 1:
                        pending.append(chain(st_ // 2))
                        for stg_ in pending:
                            if stg_:
                                stg_.pop(0)()
                if g + 1 < NG:
                    load_dw(g + 1)
            while any(pending):
                for stg_ in pending:
                    if stg_:
                        stg_.pop(0)()

        awi = [0]

        def load_w(chunk):
            slot = awi[0] % 6
            awi[0] += 1
            dma("sp", AW[:, slot, :], watt[chunk * 128:(chunk + 1) * 128, :], ["watt"], ["aw%d" % slot], "aw%d" % slot)
            return slot

        def proj_fm(slot, tb, ncols=128, c0=0):
            b = psbank()
            for kc in range(8):
                mm(b, 512, AW[:, slot, kc * 128 + c0:kc * 128 + c0 + ncols], xn[:, kc, tb * 512:(tb + 1) * 512], kc == 0, kc == 7,
                   ["aw%d" % slot, nk(kc, tb)], prow=(0, ncols))
            return b

        def headnorm_rstd(b):
            q = sqslot()
            act(SQ[:, q, :], PS[b][:, :], AF.Square, ["ps%d" % b], ["sq%d" % q])
            b2 = psbank()
            mm(b2, 512, blk64, SQ[:, q, :], True, True, ["sq%d" % q, "cmat"])
            return rstd_from_psum(b2, 512)

        def proj_qk_norm(chunk, gcolumn, dst, dkey):
            slot = load_w(chunk)
            for tb in range(4):
                b = proj_fm(slot, tb)
                t = headnorm_rstd(b)
                stt(dst[:, tb * 512:(tb + 1) * 512], PS[b][:, :], vec[:, gcolumn:gcolumn + 1], TMP[:, t, :], ALU.mult, ALU.mult,
                    ["ps%d" % b, "tmp%d" % t, "vec"], ["%s_%d" % (dkey, tb)])

        def proj_v_tm(chunk, c0, ncols, dst_fn, dkey):
            slot = load_w(chunk)
            per = 512 // ncols
            for t0 in range(0, 16, per):
                b = psbank()
                for i in range(per):
                    tt_ = t0 + i
                    for kc in range(8):
                        mm(b, ncols, xn[:, kc, tt_ * 128:(tt_ + 1) * 128], AW[:, slot, kc * 128 + c0:kc * 128 + c0 + ncols],
                           kc == 0, kc == 7, ["aw%d" % slot, nk(kc, tt_ // 4)], col0=i * ncols)
                for i in range(per):
                    tt_ = t0 + i
                    o = dst_fn(tt_)
                    src = PS[b][:, i * ncols:(i + 1) * ncols]
                    if i % 2 == 0:
                        S.add("dve", lambda e, o=o, src=src: e.tensor_copy(out=o, in_=src), reads=["ps%d" % b],
                              writes=["%s_%d" % (dkey, tt_ // 4)])
                    else:
                        act(o, src, AF.Copy, ["ps%d" % b], ["%s_%d" % (dkey, tt_ // 4)])

        qslot = lambda i: R[:, i * 2048:(i + 1) * 2048]
        kslot = lambda i: R[:, 4096 + i * 2048:4096 + (i + 1) * 2048]
        vslot = lambda i: R[:, 8192 + i * 3072:8192 + (i + 1) * 3072].rearrange("p (t c) -> p t c", c=192)
        eslot = lambda i: R[:, 14336 + i * 512:14336 + (i + 1) * 512]
        pslot = lambda i: R[:, 16384 + i * 512:16384 + (i + 1) * 512]
        epi = [0, 0]

        def attn_core(qs, ks, vkey, blocks, mask_fn, pv, fin, accsets, skip_fn=None, nsp=2, lay=None):
            qT = qslot(qs); kT = kslot(ks)
            flat = []
            for bi, (q0, nq, kcs, tabsel) in enumerate(blocks):
                kl = [kc for kc in kcs if not (skip_fn is not None and skip_fn(q0, kc, nq))]
                for i, kc in enumerate(kl):
                    flat.append((bi, q0, nq, kc, tabsel, i == 0, i == len(kl) - 1))
            sb_of = {}

            def qk(n):
                bi, q0, nq, kc, tabsel, first, last = flat[n]
                sbk = [2 * (n % nsp), 2 * (n % nsp) + 1]
                for side in range(2):
                    if lay is None:
                        pr = slice(side * 64, (side + 1) * 64)
                        mm(sbk[side], nq, kT[pr, kc * 128:(kc + 1) * 128], qT[pr, q0:q0 + nq], True, True,
                           ["k%d_%d" % (ks, kc // 4), "q%d_%d" % (qs, q0 // 512)])
                    else:
                        mm(sbk[side], nq, lay["k"][:, kc * 128:(kc + 1) * 128], lay["q"][side][:, q0:q0 + nq], True, True,
                           ["k%d_%d" % (ks, kc // 4), "q%d_%d" % (qs, q0 // 512)])
                sb_of[n] = sbk
            LA = nsp
            for n0 in range(min(LA, len(flat))):
                qk(n0)
            pend = []
            PSL = (16384, 17408, 19456) if lay is None else lay["P"]
            ESL = (14336, 15360, 18432) if lay is None else lay["E"]
            for n in range(len(flat)):
                bi, q0, nq, kc, tabsel, first, last = flat[n]
                banks = accsets[bi % len(accsets)]
                sbk = sb_of.pop(n)
                pp = epi[1] % len(PSL); epi[1] += 1
                PP = R[:, PSL[pp]:PSL[pp] + 1024].rearrange("p (s q) -> p s q", s=2)
                SS = PSA[:, sbk[0] * 512:(sbk[0] + 2) * 512].rearrange("p (s q) -> p s q", s=2)
                skeys = ["ps%d" % sbk[0], "ps%d" % sbk[1]]
                if mask_fn is None:
                    act(PP[:, :, 0:nq], SS[:, :, 0:nq], AF.Exp, skeys, ["pp%d" % pp])
                else:
                    ep = epi[0] % len(ESL); epi[0] += 1
                    EE = R[:, ESL[ep]:ESL[ep] + 1024].rearrange("p (s q) -> p s q", s=2)
                    act(EE[:, :, 0:nq], SS[:, :, 0:nq], AF.Exp, skeys, ["ee%d" % ep])
                    tt("dve", PP[:, :, 0:nq], EE[:, :, 0:nq], mask_fn(tabsel, q0, kc, nq), ALU.mult, ["ee%d" % ep, "mask"], ["pp%d" % pp])
                used_fb = False
                if pend:
                    for f_ in pend.pop(0):
                        used_fb = bool(f_(sbk[0])) or used_fb
                if last:
                    while pend:
                        for f_ in pend.pop(0):
                            f_(sbk[0])
                if n + LA < len(flat) and not used_fb:
                    qk(n + LA)
                for side in range(2):
                    for (ai, lfn, xr) in pv[side]:
                        mm(banks[ai], nq, lfn(kc), PP[:, side, 0:nq], first, last,
                           ["pp%d" % pp, "%s_%d" % (vkey, kc // 4)] + xr)
                if n + LA < len(flat) and used_fb:
                    qk(n + LA)
                if last:
                    stages = fin(q0, nq, banks, sbk[0])
                    for i_, stg_ in enumerate(stages):
                        if i_ < len(pend):
                            pend[i_].extend(stg_)
                        else:
                            pend.append(list(stg_))
            for lst_ in pend:
                for f_ in lst_:
                    f_(0)

        def out_proj(c_out0, half, nkc=4):
            for oc in range(8):
                slot = load_w(c_out0 + oc)
                for tb in range(4):
                    b = psbank()
                    for kc in range(nkc):
                        kk = half * 4 + kc
                        mm(b, 512, AW[:, slot, kk * 128:(kk + 1) * 128], AH[:, kc, tb * 512:(tb + 1) * 512], kc == 0, kc == nkc - 1,
                           ["aw%d" % slot, "ah%d_%d" % (kc, tb)])
                    stt(xT[:, oc, tb * 512:(tb + 1) * 512], PS[b][:, :], 1.0, xT[:, oc, tb * 512:(tb + 1) * 512], ALU.mult, ALU.add,
                        ["ps%d" % b, xk(oc, tb)], [xk(oc, tb)])

        def recip(t, nq, scr):
            S.add("dve", lambda e, t=t, nq=nq: e.reciprocal(out=TMP[:, t, 0:nq], in_=TMP[:, t, 0:nq]),
                  reads=["tmp%d" % t], writes=["tmp%d" % t])

        def fin_pair(cj):
            def fin(q0, nq, banks, fb):
                t = tmp()
                tbq = q0 // 512

                def s0(fb_):
                    act(TMP[64:128, t, 0:nq], PS[banks[0]][64:128, 0:nq], AF.Copy, ["ps%d" % banks[0]], ["tmp%d" % t])
                    act(TMP[0:64, t, 0:nq], PS[banks[1]][0:64, 0:nq], AF.Copy, ["ps%d" % banks[1]], ["tmp%d" % t])

                def s1(fb_):
                    recip(t, nq, None)

                def s2(fb_):
                    tt("dve", AH[0:64, cj, q0:q0 + nq], PS[banks[0]][0:64, 0:nq], TMP[64:128, t, 0:nq], ALU.mult,
                       ["ps%d" % banks[0], "tmp%d" % t], ["ah%d_%d" % (cj, tbq)])
                    tt("dve", AH[64:128, cj, q0:q0 + nq], PS[banks[1]][64:128, 0:nq], TMP[0:64, t, 0:nq], ALU.mult,
                       ["ps%d" % banks[1], "tmp%d" % t], ["ah%d_%d" % (cj, tbq)])
                return [[s0], [s1], [s2]]
            return fin

        def fin_pair1(cj, act_recip=False):
            def fin(q0, nq, banks, fb):
                tr, to = tmp(), tmp()
                tbq = q0 // 512
                f0 = AF.Ln if act_recip else AF.Copy

                def s0(fb_):
                    act(TMP[0:64, tr, 0:nq], PS[banks[0]][64:128, 0:nq], f0, ["ps%d" % banks[0]], ["tmp%d" % tr])
                    act(TMP[64:128, tr, 0:nq], PS[banks[1]][0:64, 0:nq], f0, ["ps%d" % banks[1]], ["tmp%d" % tr])
                    S.add("dve", lambda e: e.tensor_copy(out=TMP[0:64, to, 0:nq], in_=PS[banks[0]][0:64, 0:nq]),
                          reads=["ps%d" % banks[0]], writes=["tmp%d" % to])
                    S.add("dve", lambda e: e.tensor_copy(out=TMP[64:128, to, 0:nq], in_=PS[banks[1]][64:128, 0:nq]),
                          reads=["ps%d" % banks[1]], writes=["tmp%d" % to])

                def s1(fb_):
                    if act_recip:
                        act(TMP[:, tr, 0:nq], TMP[:, tr, 0:nq], AF.Exp, ["tmp%d" % tr], ["tmp%d" % tr], scale=-1.0)
                    else:
                        recip(tr, nq, None)

                def s2(fb_):
                    tt("dve", AH[:, cj, q0:q0 + nq], TMP[:, to, 0:nq], TMP[:, tr, 0:nq], ALU.mult,
                       ["tmp%d" % to, "tmp%d" % tr], ["ah%d_%d" % (cj, tbq)])
                return [[s0], [s1], [s2]]
            return fin

        FULLB = [(qb * 512, 512, list(range(16)), 0) for qb in range(4)]

        def phase_mix0():
            RK = ["R"]
            S.fence(RK + ["gu0", "gu1", "dw0", "dw1"] + ["act%d_%d_%d" % (s_, j, tb) for s_ in range(2) for j in range(2) for tb in range(4)])
            for h in range(4):
                qs = ks = vs = h % 2
                proj_qk_norm(C_AB_IN + h, V_GQ8 + 0, qslot(qs), "q%d" % qs)
                proj_qk_norm(C_AB_IN + 4 + h, V_AK, kslot(ks), "k%d" % ks)
                proj_v_tm(C_AB_IN + 8 + h, 0, 128, lambda tt_, vs=vs:
                          R[:, 8192 + vs * 3072 + tt_ * 128:8192 + vs * 3072 + (tt_ + 1) * 128], "v%d" % vs)
                dma("sp", Dtab, alibi_d[h * 128:(h + 1) * 128, :], ["mask"], ["mask"], "dtab")

                def mask_fn(tabsel, q0, kc, nq):
                    m0 = q0 - 128 * kc + 1920
                    return Dtab[:, m0:m0 + nq].unsqueeze(1).broadcast_to([128, 2, nq])
                vl = lambda kc, vs=vs: R[:, 8192 + vs * 3072 + kc * 128:8192 + vs * 3072 + (kc + 1) * 128]
                ol = lambda kc: ones1
                pv = [[(0, vl, []), (1, ol, ["cmat"])], [(2, vl, []), (3, ol, ["cmat"])]]

                def fin(q0, nq, banks, fb, h=h):
                    tbq = q0 // 512
                    tz1, tz2, to1, to2, tr = tmp(), tmp(), tmp(), tmp(), tmp()

                    def s0(fb_):
                        for t_, b_ in ((tz1, banks[1]), (tz2, banks[3])):
                            act(TMP[:, t_, :], PS[b_][:, :], AF.Ln, ["ps%d" % b_], ["tmp%d" % t_])
                        for t_, b_ in ((to1, banks[0]), (to2, banks[2])):
                            S.add("dve", lambda e, t_=t_, b_=b_: e.tensor_copy(out=TMP[:, t_, :], in_=PS[b_][:, :]),
                                  reads=["ps%d" % b_], writes=["tmp%d" % t_])

                    def s1(fb_):
                        act(TMP[:, tz1, :], TMP[:, tz1, :], AF.Exp, ["tmp%d" % tz1], ["tmp%d" % tz1], scale=-1.0)

                    def s2(fb_):
                        act(TMP[:, tz2, :], TMP[:, tz2, :], AF.Exp, ["tmp%d" % tz2], ["tmp%d" % tz2], scale=-1.0)

                    def s3(fb_):
                        tt("dve", TMP[:, to1, :], TMP[:, to1, :], TMP[:, tz1, :], ALU.mult, ["tmp%d" % to1, "tmp%d" % tz1], ["tmp%d" % to1])
                        tt("dve", TMP[:, to2, :], TMP[:, to2, :], TMP[:, tz2, :], ALU.mult, ["tmp%d" % to2, "tmp%d" % tz2], ["tmp%d" % to2])

                    sqq = [0]

                    def s4(fb_):
                        stt(TMP[:, to1, :], TMP[:, to2, :], neglam, TMP[:, to1, :], ALU.mult, ALU.add,
                            ["tmp%d" % to1, "tmp%d" % to2, "lam2"], ["tmp%d" % to1])
                        sqq[0] = sqslot()
                        act(SQ[:, sqq[0], :], TMP[:, to1, :], AF.Square, ["tmp%d" % to1], ["sq%d" % sqq[0]])

                    def s5(fb_):
                        mm(fb_, 512, ones128, SQ[:, sqq[0], :], True, True, ["sq%d" % sqq[0], "cmat"])
                        act(TMP[:, tr, :], PS[fb_][:, :], AF.Ln, ["ps%d" % fb_, "vec"], ["tmp%d" % tr], bias=eps_ap, scale=1.0)
                        act(TMP[:, tr, :], TMP[:, tr, :], AF.Exp, ["tmp%d" % tr], ["tmp%d" % tr], scale=-0.5)
                        return True

                    def s6(fb_):
                        stt(AH[:, h, q0:q0 + 512], TMP[:, to1, :], vec[:, V_SUB8:V_SUB8 + 1], TMP[:, tr, :], ALU.mult, ALU.mult,
                            ["tmp%d" % to1, "tmp%d" % tr, "vec"], ["ah%d_%d" % (h, tbq)])
                    return [[s0], [s1], [s2], [s3], [s4], [s5], [s6]]
                attn_core(qs, ks, "v%d" % vs, FULLB, mask_fn, pv, fin, [[4, 5, 6, 7]],
                          skip_fn=(None if os.environ.get("NOSKIP") else (lambda q0, kc, nq, h=h: bool(ALIBI_ZERO[h][q0 // 512][kc]))))
            out_proj(C_AB_OUT, 0)
            NAB = [(0, 256, [0, 1, 2, 3], 0), (256, 256, [0, 1, 2, 3, 4, 5], 1), (512, 512, list(range(2, 10)), 1),
                   (1024, 512, list(range(6, 14)), 1), (1536, 256, list(range(10, 16)), 1), (1792, 256, [12, 13, 14, 15], 0)]
            S.fence(["q%d_%d" % (i, t) for i in range(2) for t in range(4)] + ["k%d_%d" % (i, t) for i in range(2) for t in range(4)] +
                    ["v%d_%d" % (i, t) for i in range(2) for t in range(4)] + ["ee0", "ee1", "ee2", "pp0", "pp1", "pp2"])
            for i in range(2):
                S.add("pool", lambda e, i=i: e.memset(vslot(i)[:, :, 64:128], 1.0), reads=[], writes=["v%d_%d" % (i, t4) for t4 in range(4)])
            for j in range(4):
                qs = ks = vs = j % 2
                proj_qk_norm(C_AB_IN + 12 + j, V_GQ8 + 1, qslot(qs), "q%d" % qs)
                proj_qk_norm(C_AB_IN + 16 + j, V_BK, kslot(ks), "k%d" % ks)
                slot = load_w(C_AB_IN + 20 + j)
                for t0 in range(0, 16, 4):
                    b = psbank()
                    for i in range(4):
                        tt_ = t0 + i
                        for kc in range(8):
                            mm(b, 128, xn[:, kc, tt_ * 128:(tt_ + 1) * 128], AW[:, slot, kc * 128:(kc + 1) * 128], kc == 0, kc == 7,
                               ["aw%d" % slot, nk(kc, tt_ // 4)], col0=i * 128)
                    src = PS[b][:, :].rearrange("p (t h c) -> p t h c", t=4, h=2)
                    V = vslot(vs)
                    S.add("dve", lambda e, V=V, src=src, t0=t0: e.tensor_copy(out=V[:, t0:t0 + 4, 0:64], in_=src[:, :, 0, :]),
                          reads=["ps%d" % b], writes=["v%d_%d" % (vs, t0 // 4)])
                    act(V[:, t0:t0 + 4, 128:192], src[:, :, 1, :], AF.Copy, ["ps%d" % b, "v%d_%d" % (vs, t0 // 4)], ["v%d_%d" % (vs, t0 // 4)])
                for hh in range(2):
                    dma("sp", NAtab[:, hh * 2 * NA_TW:(hh + 1) * 2 * NA_TW], natab[(2 * j + hh) * 128:(2 * j + hh + 1) * 128, :],
                        ["natab", "mask"], ["mask"], "natb")

                def mask_fn(tabsel, q0, kc, nq):
                    mp0 = q0 // 64 - 2 * kc + 7
                    c0 = tabsel * NA_TW + (mp0 + 3) * 64
                    return NAtab.rearrange("p (s c) -> p s c", s=2)[:, :, c0:c0 + nq]
                V = vslot(vs)
                pv = [[(0, lambda kc, V=V: V[:, kc, 0:128], [])], [(1, lambda kc, V=V: V[:, kc, 64:192], [])]]
                attn_core(qs, ks, "v%d" % vs, NAB, mask_fn, pv, fin_pair1(j, act_recip=True), [[6, 7]], nsp=3)
            out_proj(C_AB_OUT, 1)

        def phase_mix1():
            S.fence(["R", "gu0", "gu1", "dw0", "dw1", "mask"] + ["act%d_%d_%d" % (s_, j, tb) for s_ in range(2) for j in range(2) for tb in range(4)])
            dma("sp", TBL[:, 0:4096], rope_d, ["mask"], ["rope"], "rope")
            for i in range(2):
                S.add("pool", lambda e, i=i: e.memset(vslot(i)[:, :, 0:64], 1.0), reads=[], writes=["v%d_%d" % (i, t4) for t4 in range(4)])
                S.add("pool", lambda e, i=i: e.memset(vslot(i)[:, :, 128:192], 1.0), reads=[], writes=["v%d_%d" % (i, t4) for t4 in range(4)])

            def proj_rope(chunk, chunk_sw, gcol, gcol_sw, dst, dkey):
                s1 = load_w(chunk)
                s2 = load_w(chunk_sw)
                for tb in range(4):
                    cols = slice(tb * 512, (tb + 1) * 512)
                    b = proj_fm(s1, tb)
                    bs = proj_fm(s2, tb)
                    t = headnorm_rstd(b)
                    t1 = tmp()
                    stt(TMP[:, t1, :], PS[b][:, :], vec[:, gcol:gcol + 1], ropeC[:, cols], ALU.mult, ALU.mult,
                        ["ps%d" % b, "vec", "rope"], ["tmp%d" % t1])
                    t2 = tmp()
                    stt(TMP[:, t2, :], PS[bs][:, :], vec[:, gcol_sw:gcol_sw + 1], ropeS[:, cols], ALU.mult, ALU.mult,
                        ["ps%d" % bs, "vec", "rope"], ["tmp%d" % t2])
                    tt(os.environ.get("ADDENG", "dve"), TMP[:, t1, :], TMP[:, t1, :], TMP[:, t2, :], ALU.add, ["tmp%d" % t1, "tmp%d" % t2], ["tmp%d" % t1])
                    tt("dve", dst[:, cols], TMP[:, t1, :], TMP[:, t, :], ALU.mult, ["tmp%d" % t1, "tmp%d" % t], ["%s_%d" % (dkey, tb)])
            CUT = int(os.environ.get("MIX1_CUT", "99"))
            if CUT <= 1:
                return
            for j in range(8):
                kv = j // 2
                qs = j % 2
                ks = vs = kv % 2
                if j % 2 == 0:
                    proj_rope(C_CK + kv, C_CKS + kv, V_CK, V_CKS, kslot(ks), "k%d" % ks)
                    if CUT <= 2:
                        return
                    V = vslot(vs)
                    proj_v_tm(C_CV + kv // 2, (kv % 2) * 64, 64, lambda tt_, V=V: V[:, tt_, 64:128], "v%d" % vs)
                    if CUT <= 3:
                        return
                VAR = os.environ.get("VARQ", "")
                if VAR == "A":
                    proj_rope(C_CK + kv, C_CKS + kv, V_CK, V_CKS, kslot(ks), "k%d" % ks)
                elif VAR == "B":
                    proj_rope(C_CQ + j, C_CQS + j, V_CK, V_CKS, qslot(qs), "q%d" % qs)
                elif VAR == "C":
                    proj_rope(C_CQ + j, C_CQS + j, V_GQ8 + 2, V_GQ8 + 3, kslot(ks), "k%d" % ks)
                else:
                    proj_rope(C_CQ + j, C_CQS + j, V_GQ8 + 2, V_GQ8 + 3, qslot(qs), "q%d" % qs)
                if CUT <= 4:
                    return
                V = vslot(vs)
                pv = [[(0, lambda kc, V=V: V[:, kc, 64:192], [])], [(1, lambda kc, V=V: V[:, kc, 0:128], [])]]
                attn_core(qs, ks, "v%d" % vs, FULLB, None, pv, fin_pair1(j % 4), [[6, 7]], nsp=3)
                if j % 4 == 3:
                    out_proj(C_C_OUT, j // 4)

        def ffn_fence():
            S.fence(["R", "mask", "rope", "stg0", "stg1", "stgi0", "stgi1"] + ["q%d_%d" % (i, t) for i in range(2) for t in range(4)] +
                    ["k%d_%d" % (i, t) for i in range(2) for t in range(4)] + ["v%d_%d" % (i, t) for i in range(2) for t in range(4)] +
                    ["ee0", "ee1", "ee2", "pp0", "pp1", "pp2"])

        ACTK = ["act%d_%d_%d" % (s_, j, tb) for s_ in range(2) for j in range(2) for tb in range(4)]
        for s in range(nseq):
            if s == 0:
                ffn_fence()
                phase_in(s)
            if stop < 99:
                ffn_fence()
                for l in range(2):
                    if stop <= 3 * l:
                        break
                    phase_norm(V_FFN1 + 8 * l, False)
                    phase_ffn(l * 2 + 0)
                    if stop <= 3 * l + 1:
                        break
                    phase_norm(V_MIX + 8 * l, False)
                    if l == 0:
                        phase_mix0()
                    else:
                        phase_mix1()
                    if stop <= 3 * l + 2:
                        break
                    ffn_fence()
                    phase_norm(V_FFN2 + 8 * l, False)
                    phase_ffn(l * 2 + 1)
                    phase_norm(V_FIN + 8 * l, True)
            else:
                phase_norm(V_FFN1, False)
                ffn_fence()
                for l in range(2):
                    if s == 0 and l == 0:
                        cast_ffn(1)
                    phase_ffn(l * 2 + 0, chain=norm_chain([(V_MIX + 8 * l, False)]))
                    if s == 0 and l == 0:
                        cast_ffn(2)
                    if l == 0:
                        phase_mix0()
                    else:
                        phase_mix1()
                    phase_norm(V_FFN2 + 8 * l, False)
                    ffn_fence()
                    if s == 0 and l == 0:
                        cast_ffn(3)
                    specs = [(V_FIN + 8 * l, True)] + ([(V_FFN1 + 8, False)] if l == 0 else [])
                    phase_ffn(l * 2 + 1, chain=norm_chain(specs))
            ffn_fence()
            S.fence(["gu0", "gu1", "dw0", "dw1"] + ACTK)
            if s + 1 < nseq and stop >= 99:
                phase_out_in(s)
            else:
                phase_out(s)
                if s + 1 < nseq:
                    phase_in(s + 1)
        S.add("sp", None, reads=["yout0", "yout1"])
        S.emit(st)
    return nc


def _chunk(w, cols):
    blk = w[:, cols]
    return np.ascontiguousarray(blk.reshape(8, 128, 128).transpose(1, 0, 2).reshape(128, 1024))


def _consts():
    identf = np.eye(128, dtype=np.float32)
    cm = np.zeros((128, 512), np.float32)
    cm[:, 0:128] = 1.0 / 1024
    cm[0:64, 128:192] = 1.0 / 64
    cm[64:128, 192:256] = 1.0 / 64
    cm[:, 256:384] = 1.0 / 128
    cm[:, 384:512] = 1.0
    cmat = cm.astype(ml_dtypes.bfloat16)
    p = np.arange(128)[:, None]
    m = np.arange(DW)[None, :]
    dist = np.abs(m - p - 1920).astype(np.float64)
    slopes = 2.0 ** (-8.0 * np.arange(1, 5) / 4)
    alibi = np.concatenate([np.exp(-sl * dist) for sl in slopes], axis=0).astype(np.float32).astype(ml_dtypes.bfloat16)
    f = np.arange(128) % 64
    mm_ = f % 16
    inv_freq = (10000.0 ** (-(np.arange(0, 32, 2, dtype=np.float32)) / 32)).astype(np.float32)
    t = np.arange(S_LEN)
    pos = np.where((f[:, None] // 32) == 0, (t[None, :] // 64), (t[None, :] % 64)).astype(np.float32)
    ang = pos * inv_freq[mm_][:, None]
    sign = np.where((f % 32) < 16, -1.0, 1.0)[:, None]
    rope = np.concatenate([np.cos(ang), sign * np.sin(ang)], axis=1).astype(np.float32)
    pp = np.arange(128)
    half = pp // 64
    ck = pp % 64
    mi = np.arange(NA_NM)
    cq = np.arange(64)
    dr = half[:, None, None] + 7 - (mi[None, :, None] - 3)
    dc = ck[:, None, None] - cq[None, None, :]
    cstart = np.clip(cq - 8, 0, 48)[None, None, :]
    colv = (ck[:, None, None] >= cstart) & (ck[:, None, None] < cstart + 16)
    vF = (np.abs(dr) <= 7) & colv
    vI = (dr >= -4) & (dr <= 3) & colv
    navalid = np.concatenate([vF.reshape(128, -1), vI.reshape(128, -1)], axis=1).astype(np.float32).astype(ml_dtypes.bfloat16)
    dr_b = np.broadcast_to(dr, vF.shape)
    dc_b = np.broadcast_to(dc, vF.shape)
    return identf, cmat, alibi, rope, navalid, vF, dr_b, dc_b


def _prep(inp):
    g = lambda k: np.asarray(inp[k], dtype=np.float32)
    blocks = []
    for l in range(2):
        for nm in ("ffn1", "ffn2"):
            wg = g(nm + "_w_gate")[l].reshape(8, 128, NG, 256).transpose(2, 1, 0, 3).reshape(NG, 128, 2048)
            wu = g(nm + "_w_up")[l].reshape(8, 128, NG, 256).transpose(2, 1, 0, 3).reshape(NG, 128, 2048)
            wd = g(nm + "_w_down")[l].reshape(NG, 2, 128, 1024).transpose(0, 2, 1, 3).reshape(NG, 128, 2048)
            blocks.append(np.concatenate([wg, wu, wd], axis=2))
    wffn = np.ascontiguousarray(np.stack(blocks).reshape(4 * NG * 128, 6144))
    ch = []
    abin = g("ab_w_in")[0]
    for c in range(24):
        ch.append(_chunk(abin, slice(c * 128, (c + 1) * 128)))
    abo = g("ab_w_out")[0]
    for c in range(8):
        ch.append(_chunk(abo, slice(c * 128, (c + 1) * 128)))
    cin = g("c_w_in")[0]
    sw = np.arange(1024) ^ 16
    wq = cin[:, 0:1024]
    for c in range(8):
        ch.append(_chunk(wq, slice(c * 128, (c + 1) * 128)))
    wqs = wq[:, sw]
    for c in range(8):
        ch.append(_chunk(wqs, slice(c * 128, (c + 1) * 128)))
    wk = cin[:, 1024:1280]
    wks = wk[:, np.arange(256) ^ 16]
    for src in (wk, wks):
        for kv in range(4):
            cols = np.concatenate([np.arange(kv * 64, kv * 64 + 64)] * 2)
            ch.append(_chunk(src, cols))
    wv = cin[:, 1280:1536]
    for c in range(2):
        ch.append(_chunk(wv, slice(c * 128, (c + 1) * 128)))
    co = g("c_w_out")[0]
    for c in range(8):
        ch.append(_chunk(co, slice(c * 128, (c + 1) * 128)))
    watt = np.ascontiguousarray(np.concatenate(ch, axis=0))
    assert watt.shape == (N_ATT * 128, 1024)
    vec = np.zeros((128, NVEC), np.float32)
    for l in range(2):
        for base, nm in ((V_FFN1, "ffn1_norm"), (V_MIX, "mix_norm"), (V_FFN2, "ffn2_norm"), (V_FIN, "final_norm")):
            vec[:, base + 8 * l:base + 8 * l + 8] = g(nm)[l].reshape(8, 128).T
    t2 = lambda v: np.concatenate([v, v])
    vec[:, V_AQ] = t2(g("a_q_norm")[0]); vec[:, V_AK] = t2(g("a_k_norm")[0])
    vec[:, V_BQ] = t2(g("b_q_norm")[0]); vec[:, V_BK] = t2(g("b_k_norm")[0])
    vec[:, V_SUB] = g("a_sub_norm")[0]
    s64 = np.arange(64) ^ 16
    vec[:, V_CQ] = t2(g("c_q_norm")[0]); vec[:, V_CQS] = t2(g("c_q_norm")[0][s64])
    vec[:, V_CK] = t2(g("c_k_norm")[0]); vec[:, V_CKS] = t2(g("c_k_norm")[0][s64])
    lam = np.concatenate([g("a_lambda_q1")[0], g("a_lambda_k1")[0], g("a_lambda_q2")[0], g("a_lambda_k2")[0]])
    lamv = np.ascontiguousarray(np.broadcast_to(lam[None, :], (128, 256)))
    identf, cmat, alibi, rope, navalid, vF, dr_b, dc_b = _consts()
    rpb = g("b_rpb")[0]
    rpbt = np.zeros((8,) + vF.shape, np.float32)
    idx = np.nonzero(vF)
    for h in range(8):
        rpbt[h][idx] = rpb[h][dr_b[idx] + 7, dc_b[idx] + 15]
    rpbt = np.ascontiguousarray(rpbt.reshape(8 * 128, NA_TW))
    return dict(wffn=wffn, watt=watt, vec=vec, lamv=lamv, identf=identf, cmat=cmat, alibi=alibi, rope=rope,
                rpbt=rpbt, navalid=navalid)


def kernel(**inputs):
    xp = np.asarray(inputs["x_prompt"], dtype=np.float32)
    xs = np.asarray(inputs["x_sample"], dtype=np.float32)
    shared = _prep(inputs)
    npc, nsc = xp.shape[0] // NCORE, xs.shape[0] // NCORE
    nseq = npc + nsc
    nc = build(nseq)
    in_maps = []
    for i in range(NCORE):
        xi = np.concatenate([xp[i * npc:(i + 1) * npc], xs[i * nsc:(i + 1) * nsc]], axis=0).reshape(nseq * S_LEN, D)
        m = dict(shared)
        m["xin"] = np.ascontiguousarray(xi)
        in_maps.append(m)
    res = run_bass_kernel_spmd(nc, in_maps, core_ids=list(range(NCORE)))
    yp = np.empty_like(xp)
    ys = np.empty_like(xs)
    for i in range(NCORE):
        y = np.asarray(res.results[i]["yout"]).reshape(nseq, S_LEN, D)
        yp[i * npc:(i + 1) * npc] = y[:npc]
        ys[i * nsc:(i + 1) * nsc] = y[npc:]
    return (yp, ys)
```

```python
import math
import os
import numpy as np
from contextlib import ExitStack
import ml_dtypes
import concourse.bass as bass
import concourse.mybir as mybir
from concourse.bass_utils import run_bass_kernel_spmd

F32 = mybir.dt.float32
BF16 = mybir.dt.bfloat16
ALU = mybir.AluOpType
AF = mybir.ActivationFunctionType

D = 1024
S_LEN = 2048
DFF = 2816
NG = 11
EPS = 1e-6
NCORE = 8
NA_NM = 22
NA_TW = NA_NM * 64
DW = 3968
LAMBDA_INIT0 = 0.8 - 0.6 * math.exp(-0.3 * 0)

C_AB_IN = 0
C_AB_OUT = 24
C_CQ = 32
C_CQS = 40
C_CK = 48
C_CKS = 52
C_CV = 56
C_C_OUT = 58
N_ATT = 66

V_FFN1 = 0
V_MIX = 16
V_FFN2 = 32
V_FIN = 48
V_AQ = 64
V_AK = 65
V_BQ = 66
V_BK = 67
V_SUB = 68
V_CQ = 69
V_CQS = 70
V_CK = 71
V_CKS = 72
NVEC = 73


class Op:
    __slots__ = ("eng", "fn", "idx", "signal", "waits", "dma_key", "dma_cnt", "vc", "sigcnt", "dwaits")


class Sched:
    ENGS = ("pe", "act", "dve", "pool", "sp")
    EPOCH = 30000

    def __init__(self, nc):
        self.nc = nc
        self.ops = {e: [] for e in self.ENGS}
        self.last_w = {}
        self.real_w = {}
        self.readers = {}
        self.known = {e: {} for e in self.ENGS}
        self.known_dma = {e: {} for e in self.ENGS}
        self.dma_cnt = {}

    def add(self, eng, fn, reads=(), writes=(), dma=None, record=True):
        op = Op()
        op.eng = eng; op.fn = fn; op.signal = False; op.waits = []; op.dwaits = []
        op.dma_key = dma; op.dma_cnt = 0
        lst = self.ops[eng]
        op.idx = len(lst) + 1
        deps = []
        raw = set()
        ps_reads = [r for r in reads if r.startswith("ps")] if fn is not None else []
        true_writes = writes
        if ps_reads:
            writes = list(writes) + ps_reads
        for r in reads:
            w = self.last_w.get(r)
            if w is not None:
                deps.append(w)
                if self.real_w.get(r) is w:
                    raw.add(id(w))
        for w_ in writes:
            lw = self.last_w.get(w_)
            if lw is not None:
                deps.append(lw)
            rl = self.readers.get(w_)
            if rl:
                deps.extend(rl)
        known = self.known[eng]
        kdma = self.known_dma[eng]
        for y in deps:
            if y is op:
                continue
            if y.dma_key is not None:
                if kdma.get(y.dma_key, 0) >= y.dma_cnt:
                    continue
                kdma[y.dma_key] = y.dma_cnt
                op.dwaits.append((y.dma_key, y.dma_cnt))
                for k, v in y.vc.items():
                    if known.get(k, 0) < v:
                        known[k] = v
                continue
            if y.eng == eng:
                if eng == "pe" or eng == "sp":
                    continue
                if id(y) not in raw:
                    continue
            if known.get(y.eng, 0) >= y.idx:
                continue
            y.signal = True
            op.waits.append(y)
            for k, v in y.vc.items():
                if known.get(k, 0) < v:
                    known[k] = v
            known[y.eng] = y.idx
        op.vc = dict(known)
        if dma is not None:
            c = self.dma_cnt.get(dma, 0) + 16
            self.dma_cnt[dma] = c
            op.dma_cnt = c
        if record and fn is not None:
            for r in reads:
                self.readers.setdefault(r, []).append(op)
            for w_ in writes:
                self.last_w[w_] = op
                self.readers[w_] = []
            for w_ in true_writes:
                self.real_w[w_] = op
        lst.append(op)
        return op

    def fence(self, keys):
        keys = list(keys)
        for e in self.ENGS:
            self.add(e, None, reads=keys, writes=keys, record=False)

    def emit(self, stack):
        nc = self.nc
        sems = {}

        def sem(name):
            if name not in sems:
                sems[name] = stack.enter_context(nc.semaphore("s_" + name.replace(":", "_")))
            return sems[name]
        for e in self.ENGS:
            c = 0
            for op in self.ops[e]:
                if op.signal:
                    c += 1
                    op.sigcnt = c
        for e in self.ENGS:
            for op in self.ops[e]:
                if op.signal:
                    sem("%s:%d" % (e, (op.sigcnt - 1) // self.EPOCH))
                if op.dma_key is not None:
                    sem("d:" + str(op.dma_key))
        block = stack.enter_context(nc.Block())
        EP = self.EPOCH

        def run(engobj, e):
            for op in self.ops[e]:
                for y in op.waits:
                    ep = (y.sigcnt - 1) // EP
                    engobj.wait_ge(sem("%s:%d" % (y.eng, ep)), y.sigcnt - ep * EP)
                for key, cnt in op.dwaits:
                    engobj.wait_ge(sem("d:" + str(key)), cnt)
                if op.fn is None:
                    continue
                ins = op.fn(engobj)
                if op.dma_key is not None:
                    ins.then_inc(sem("d:" + str(op.dma_key)), 16)
                elif op.signal:
                    ep = (op.sigcnt - 1) // EP
                    ins.then_inc(sem("%s:%d" % (e, ep)), 1)

        @block.tensor
        def _(eng): run(eng, "pe")

        @block.scalar
        def _(eng): run(eng, "act")

        @block.vector
        def _(eng): run(eng, "dve")

        @block.gpsimd
        def _(eng): run(eng, "pool")

        @block.sync
        def _(eng): run(eng, "sp")


_ALIBI_ZERO = []


def _alibi_zero():
    if not _ALIBI_ZERO:
        al = np.asarray(_consts()[2]).astype(np.float32).reshape(4, 128, DW)
        z = np.zeros((4, 4, 16), bool)
        for h in range(4):
            for qb in range(4):
                for kc in range(16):
                    m0 = qb * 512 - 128 * kc + 1920
                    z[h, qb, kc] = not np.any(al[h][:, m0:m0 + 512])
        _ALIBI_ZERO.append(z)
    return _ALIBI_ZERO[0]


def build(nseq, stop=99):
    ALIBI_ZERO = _alibi_zero()
    nc = bass.Bass("TRN2", target_bir_lowering=False)
    dt = lambda n, s, d, k: nc.dram_tensor(n, s, d, kind=k).ap()
    xin = dt("xin", [nseq * S_LEN, D], F32, "ExternalInput")
    yout = dt("yout", [nseq * S_LEN, D], F32, "ExternalOutput")
    wffn32 = dt("wffn", [4 * NG * 128, 6144], F32, "ExternalInput")
    watt32 = dt("watt", [N_ATT * 128, 1024], F32, "ExternalInput")
    vec_d = dt("vec", [128, NVEC], F32, "ExternalInput")
    lamv_d = dt("lamv", [128, 256], F32, "ExternalInput")
    identf_d = dt("identf", [128, 128], F32, "ExternalInput")
    cmat_d = dt("cmat", [128, 512], BF16, "ExternalInput")
    alibi_d = dt("alibi", [4 * 128, DW], BF16, "ExternalInput")
    rope_d = dt("rope", [128, 4096], F32, "ExternalInput")
    rpbt_d = dt("rpbt", [8 * 128, NA_TW], F32, "ExternalInput")
    navalid_d = dt("navalid", [128, 2 * NA_TW], BF16, "ExternalInput")
    wffn = dt("wffn_b", [4 * NG * 128, 6144], BF16, "Internal")
    watt = dt("watt_b", [N_ATT * 128, 1024], BF16, "Internal")
    natab = dt("natab", [8 * 128, 2 * NA_TW], BF16, "Internal")

    st = ExitStack()
    with st:
        sb = lambda n, s, d: st.enter_context(nc.sbuf_tensor(n, s, d))
        xT = sb("xT", [128, 8, S_LEN], F32)
        xn = sb("xn", [128, 8, S_LEN], BF16)
        R = sb("R", [128, 20480], BF16)
        TMP = sb("TMP", [128, 8, 512], F32)
        SQ = sb("SQ", [128, 4, 512], BF16)
        AH = sb("AH", [128, 4, S_LEN], BF16)
        AW = sb("AW", [128, 6, 1024], BF16)
        TBL = sb("TBL", [128, 5120], F32)
        identf = sb("identf_s", [128, 128], F32)
        cmat = sb("cmat_s", [128, 512], BF16)
        vec = sb("vec_s", [128, NVEC + 8], F32)
        lamt = sb("lamt", [128, 256], F32)
        lam2 = sb("lam2", [128, 8], F32)
        PSA = st.enter_context(nc.psum_tensor("psa", [128, 4096], F32))
        PS = [PSA[:, i * 512:(i + 1) * 512] for i in range(8)]
        S = Sched(nc)

        ones1024 = cmat[:, 0:128]
        blk64 = cmat[:, 128:256]
        ones128 = cmat[:, 256:384]
        ones1 = cmat[:, 384:512]
        eps_ap = vec[:, NVEC:NVEC + 1]
        V_GQ8 = NVEC + 1
        neglam = lam2[:, 4:5]
        V_SUB8 = NVEC + 5

        TBLb = TBL[:].bitcast(BF16)
        Dtab = TBLb[:, 0:DW]
        NAtab = TBLb[:, 4096:4096 + 4 * NA_TW]
        ropeC = TBL[:, 0:2048]
        ropeS = TBL[:, 2048:4096]
        Rf = R[:].bitcast(F32)

        psi = [0]

        def psbank(lo=0, hi=8):
            b = lo + psi[0] % (hi - lo)
            psi[0] += 1
            return b

        def mm(b, ncol, lhsT, rhs, start, stop, reads, col0=0, prow=None):
            out = PS[b][:, col0:col0 + ncol] if prow is None else PS[b][prow[0]:prow[1], col0:col0 + ncol]
            S.add("pe", lambda e, o=out, l=lhsT, r=rhs, s=start, p=stop: e.matmul(o, l, r, start=s, stop=p),
                  reads=reads, writes=["ps%d" % b])

        def act(out, in_, func, reads, writes, bias=None, scale=None):
            kw = {}
            if bias is not None:
                kw["bias"] = bias
            if scale is not None:
                kw["scale"] = scale
            S.add("act", lambda e, o=out, i=in_, f=func, kw=kw: e.activation(out=o, in_=i, func=f, **kw),
                  reads=reads, writes=writes)

        def tt(eng, out, in0, in1, op, reads, writes):
            S.add(eng, lambda e, o=out, a=in0, b=in1, p=op: e.tensor_tensor(out=o, in0=a, in1=b, op=p),
                  reads=reads, writes=writes)

        def stt(out, in0, scalar, in1, op0, op1, reads, writes):
            S.add("dve", lambda e, o=out, a=in0, s=scalar, b=in1, p0=op0, p1=op1:
                  e.scalar_tensor_tensor(out=o, in0=a, scalar=s, in1=b, op0=p0, op1=p1),
                  reads=reads, writes=writes)

        def dma(q, out, in_, reads, writes, key):
            S.add(q, lambda e, o=out, i=in_: e.dma_start(out=o, in_=i), reads=reads, writes=writes, dma=key)

        tmpi = [0]

        def tmp():
            i = tmpi[0] % 8
            tmpi[0] += 1
            return i

        sqi = [0]

        def sqslot():
            i = sqi[0] % 4
            sqi[0] += 1
            return i

        def rstd_from_psum(b, ncol, extra_reads=()):
            t = tmp()
            act(TMP[:, t, 0:ncol], PS[b][:, 0:ncol], AF.Ln, ["ps%d" % b, "vec"], ["tmp%d" % t], bias=eps_ap, scale=1.0)
            act(TMP[:, t, 0:ncol], TMP[:, t, 0:ncol], AF.Exp, ["tmp%d" % t], ["tmp%d" % t], scale=-0.5)
            return t

        dma("sp", identf[:], identf_d, [], ["identf"], "c_id")
        dma("sp", cmat[:], cmat_d, [], ["cmat"], "c_cm")
        dma("sp", vec[:, 0:NVEC], vec_d, [], ["vec"], "c_vec")
        dma("sp", lamt[:], lamv_d, [], ["lamt"], "c_lam")
        S.add("pool", lambda e: e.memset(vec[:, NVEC:NVEC + 1], EPS), reads=["vec"], writes=["vec"])
        for i, c in enumerate((V_AQ, V_BQ, V_CQ, V_CQS)):
            S.add("dve", lambda e, i=i, c=c: e.tensor_scalar(out=vec[:, V_GQ8 + i:V_GQ8 + i + 1], in0=vec[:, c:c + 1],
                                                            scalar1=0.125, scalar2=None, op0=ALU.mult),
                  reads=["vec"], writes=["vec"])
        S.add("dve", lambda e: e.tensor_scalar(out=vec[:, V_SUB8:V_SUB8 + 1], in0=vec[:, V_SUB:V_SUB + 1],
                                               scalar1=1.0 - LAMBDA_INIT0, scalar2=None, op0=ALU.mult),
              reads=["vec"], writes=["vec"])
        S.add("dve", lambda e: e.tensor_tensor(out=lamt[:, 0:64], in0=lamt[:, 0:64], in1=lamt[:, 64:128], op=ALU.mult),
              reads=["lamt"], writes=["lamt"])
        S.add("dve", lambda e: e.tensor_tensor(out=lamt[:, 128:192], in0=lamt[:, 128:192], in1=lamt[:, 192:256], op=ALU.mult),
              reads=["lamt"], writes=["lamt"])
        S.add("dve", lambda e: e.reduce_sum(out=lam2[:, 0:1], in_=lamt[:, 0:64], axis=mybir.AxisListType.X),
              reads=["lamt"], writes=["lam2"])
        S.add("dve", lambda e: e.reduce_sum(out=lam2[:, 1:2], in_=lamt[:, 128:192], axis=mybir.AxisListType.X),
              reads=["lamt", "lam2"], writes=["lam2"])
        act(lam2[:, 2:4], lam2[:, 0:2], AF.Exp, ["lam2"], ["lam2"])
        S.add("dve", lambda e: e.scalar_tensor_tensor(out=lam2[:, 4:5], in0=lam2[:, 3:4], scalar=-LAMBDA_INIT0, in1=lam2[:, 2:3],
                                                      op0=ALU.add, op1=ALU.subtract),
              reads=["lam2"], writes=["lam2"])
        def cast_ffn(fi):
            for i in range(fi * NG, (fi + 1) * NG):
                key = "wffn0_g%d" % (i - fi * NG) if fi == 0 else "wffn%d" % fi
                sem_ = "cw0_%d" % (i - fi * NG) if fi == 0 else "cw%d" % fi
                dma("pool", wffn[i * 128:(i + 1) * 128, :], wffn32[i * 128:(i + 1) * 128, :], [], [key], sem_)
        cast_ffn(0)
        for i in range(0, N_ATT, 6):
            dma("pool", watt[i * 128:(i + 6) * 128, :], watt32[i * 128:(i + 6) * 128, :], [], ["watt"], "ca")
        nav = R[:, 0:2 * NA_TW]
        dma("sp", nav, navalid_d, [], ["nav"], "c_nav")
        for h in range(8):
            dma("sp", TBL[:, 0:NA_TW], rpbt_d[h * 128:(h + 1) * 128, :], ["tbl_e"], ["tbl_r"], "c_rp")
            act(TBL[:, 2048:2048 + NA_TW], TBL[:, 0:NA_TW], AF.Exp, ["tbl_r"], ["tbl_e"])
            ob = TBLb[:, 7168:7168 + 2 * NA_TW]
            for k in range(2):
                tt("dve", ob[:, k * NA_TW:(k + 1) * NA_TW], TBL[:, 2048:2048 + NA_TW], nav[:, k * NA_TW:(k + 1) * NA_TW],
                   ALU.mult, ["tbl_e", "nav", "tbl_o"], ["tbl_o"])
            dma("sp", natab[h * 128:(h + 1) * 128, :], ob, ["tbl_o"], ["natab"], "c_nt")
        S.fence(["nav", "tbl_r", "tbl_e", "tbl_o", "R", "TBL"])

        XK = ["xT%d_%d" % (c, tb) for c in range(8) for tb in range(4)]
        xk = lambda c, tb: "xT%d_%d" % (c, tb)
        nk = lambda c, tb: "xn%d_%d" % (c, tb)

        def in_tile(s, tt_, q="sp", base=0, pfx="stg", spfx="ld"):
            slot = tt_ % 2
            stg = Rf[:, base + slot * 1024:base + (slot + 1) * 1024]
            dma(q, stg, xin[(s * 16 + tt_) * 128:(s * 16 + tt_ + 1) * 128, :], [], ["%s%d" % (pfx, slot)], "%s%d" % (spfx, slot))
            for half in range(2):
                b = psbank()
                for c4 in range(4):
                    c = half * 4 + c4
                    S.add("pe", lambda e, b=b, c4=c4, c=c, stg=stg: e.transpose(PS[b][:, c4 * 128:(c4 + 1) * 128],
                                                                              stg[:, c * 128:(c + 1) * 128], identf[:]),
                          reads=["%s%d" % (pfx, slot), "identf"], writes=["ps%d" % b])
                tb = tt_ // 4
                o = xT[:, half * 4:half * 4 + 4, tt_ * 128:(tt_ + 1) * 128]
                i = PS[b][:, :].rearrange("p (c t) -> p c t", c=4)
                wr = [xk(half * 4 + c4, tb) for c4 in range(4)]
                if half == 0:
                    S.add("dve", lambda e, o=o, i=i: e.tensor_copy(out=o, in_=i), reads=["ps%d" % b], writes=wr)
                else:
                    act(o, i, AF.Copy, ["ps%d" % b], wr)

        def out_tile(s, tt_):
            slot = tt_ % 2
            stg = Rf[:, slot * 1024:(slot + 1) * 1024]
            tb = tt_ // 4
            for half in range(2):
                b = psbank()
                for c4 in range(4):
                    c = half * 4 + c4
                    S.add("pe", lambda e, b=b, c4=c4, c=c, tt_=tt_: e.transpose(PS[b][:, c4 * 128:(c4 + 1) * 128],
                                                                      xT[:, c, tt_ * 128:(tt_ + 1) * 128], identf[:]),
                          reads=[xk(c, tb), "identf"], writes=["ps%d" % b])
                o = stg[:, half * 512:(half + 1) * 512]
                if half == 0:
                    S.add("dve", lambda e, o=o, b=b: e.tensor_copy(out=o, in_=PS[b][:, :]), reads=["ps%d" % b],
                          writes=["stg%d" % slot])
                else:
                    act(o, PS[b][:, :], AF.Copy, ["ps%d" % b], ["stg%d" % slot])
            dma("sp", yout[(s * 16 + tt_) * 128:(s * 16 + tt_ + 1) * 128, :], stg, ["stg%d" % slot], ["yout%d" % slot], "st%d" % slot)

        def phase_in(s):
            for tt_ in range(16):
                in_tile(s, tt_)

        def phase_out(s):
            for tt_ in range(16):
                out_tile(s, tt_)

        def phase_out_in(s):
            for i_ in range(20):
                if i_ < 16:
                    out_tile(s, i_)
                if i_ >= 4:
                    in_tile(s + 1, i_ - 4, q="pool", base=2048, pfx="stgi", spfx="ldi")

        def norm_stats(tb):
            cols = slice(tb * 512, (tb + 1) * 512)
            b = psbank()
            for c in range(8):
                q = sqslot()
                if c % 2 == 0:
                    act(SQ[:, q, :], xT[:, c, cols], AF.Square, [xk(c, tb)], ["sq%d" % q])
                else:
                    tt("pool", SQ[:, q, :], xT[:, c, cols], xT[:, c, cols], ALU.mult, [xk(c, tb)], ["sq%d" % q])
                mm(b, 512, ones1024, SQ[:, q, :], c == 0, c == 7, ["sq%d" % q, "cmat"])
            return rstd_from_psum(b, 512)

        def norm_apply(gcol, inplace, tb, t):
            cols = slice(tb * 512, (tb + 1) * 512)
            for c in range(8):
                g = vec[:, gcol + c:gcol + c + 1]
                if inplace:
                    stt(xT[:, c, cols], xT[:, c, cols], g, TMP[:, t, :], ALU.mult, ALU.mult,
                        [xk(c, tb), "tmp%d" % t, "vec"], [xk(c, tb)])
                else:
                    stt(xn[:, c, cols], xT[:, c, cols], g, TMP[:, t, :], ALU.mult, ALU.mult,
                        [xk(c, tb), "tmp%d" % t, "vec"], [nk(c, tb)])

        def phase_norm(gcol, inplace):
            for tb in range(4):
                t = norm_stats(tb)
                norm_apply(gcol, inplace, tb, t)

        def norm_chain(specs):
            def make(tb):
                hold = [None]
                stages = []

                def st_first():
                    hold[0] = norm_stats(tb)
                stages.append(st_first)
                for i, (gcol, inplace) in enumerate(specs):
                    def st_(i=i, gcol=gcol, inplace=inplace):
                        norm_apply(gcol, inplace, tb, hold[0])
                        if i + 1 < len(specs):
                            hold[0] = norm_stats(tb)
                    stages.append(st_)
                return stages
            return make

        def phase_ffn(fi, chain=None, preloaded=False):
            gus = lambda slot: R[:, slot * 4096:(slot + 1) * 4096]
            dws = lambda slot: R[:, 8192 + slot * 2048:8192 + (slot + 1) * 2048]
            ab = lambda slot: R[:, 12288 + slot * 4096:12288 + (slot + 1) * 4096]

            wkey = lambda g: ("wffn0_g%d" % g) if fi == 0 else ("wffn%d" % fi)

            def load_gu(g):
                slot = g % 2
                rows = slice((fi * NG + g) * 128, (fi * NG + g + 1) * 128)
                dma("sp", gus(slot), wffn[rows, 0:4096], [wkey(g)], ["gu%d" % slot], "gu%d" % slot)

            def load_dw(g):
                slot = g % 2
                rows = slice((fi * NG + g) * 128, (fi * NG + g + 1) * 128)
                dma("sp", dws(slot), wffn[rows, 4096:6144], [wkey(g)], ["dw%d" % slot], "dw%d" % slot)

            def up_step(g, tb, j):
                slot = g % 2
                w = gus(slot)
                a = ab(slot)
                cols = slice(tb * 512, (tb + 1) * 512)
                bg = psbank()
                for kc in range(8):
                    mm(bg, 512, w[:, kc * 256 + j * 128:kc * 256 + (j + 1) * 128], xn[:, kc, cols], kc == 0, kc == 7,
                       ["gu%d" % slot, nk(kc, tb)])
                bu = psbank()
                for kc in range(8):
                    mm(bu, 512, w[:, 2048 + kc * 256 + j * 128:2048 + kc * 256 + (j + 1) * 128], xn[:, kc, cols],
                       kc == 0, kc == 7, ["gu%d" % slot, nk(kc, tb)])
                t = tmp()
                act(TMP[:, t, :], PS[bg][:, :], AF.Silu, ["ps%d" % bg], ["tmp%d" % t])
                tt("dve", a[:, j * 2048 + tb * 512:j * 2048 + (tb + 1) * 512], PS[bu][:, :], TMP[:, t, :], ALU.mult,
                   ["ps%d" % bu, "tmp%d" % t], ["act%d_%d_%d" % (slot, j, tb)])

            def down_step(g, tb, oc):
                slot = g % 2
                w = dws(slot)
                a = ab(slot)
                cols = slice(tb * 512, (tb + 1) * 512)
                b = psbank()
                for j in range(2):
                    mm(b, 512, w[:, j * 1024 + oc * 128:j * 1024 + (oc + 1) * 128],
                       a[:, j * 2048 + tb * 512:j * 2048 + (tb + 1) * 512], j == 0, j == 1,
                       ["dw%d" % slot, "act%d_%d_%d" % (slot, j, tb)])
                stt(xT[:, oc, cols], PS[b][:, :], 0.5, xT[:, oc, cols], ALU.mult, ALU.add,
                    ["ps%d" % b, xk(oc, tb)], [xk(oc, tb)])
            if fi == "prefetch":
                return load_gu, load_dw
            if not preloaded:
                load_gu(0)
                load_dw(0)
            pending = []
            for g in range(NG + 1):
                if g + 1 < NG:
                    load_gu(g + 1)
                for st_ in range(8):
                    if g < NG:
                        up_step(g, st_ // 2, st_ % 2)
                    if g >= 1:
                        for k in range(4):
                            down_step(g - 1, st_ // 2, (st_ % 2) * 4 + k)
                    if g == NG and chain is not None and st_ % 2 == 1:
                        pending.append(chain(st_ // 2))
                        for stg_ in pending:
                            if stg_:
                                stg_.pop(0)()
                if g + 1 < NG:
                    load_dw(g + 1)
            while any(pending):
                for stg_ in pending:
                    if stg_:
                        stg_.pop(0)()

        awi = [0]

        def load_w(chunk):
            slot = awi[0] % 6
            awi[0] += 1
            dma("sp", AW[:, slot, :], watt[chunk * 128:(chunk + 1) * 128, :], ["watt"], ["aw%d" % slot], "aw%d" % slot)
            return slot

        def proj_fm(slot, tb, ncols=128, c0=0):
            b = psbank()
            for kc in range(8):
                mm(b, 512, AW[:, slot, kc * 128 + c0:kc * 128 + c0 + ncols], xn[:, kc, tb * 512:(tb + 1) * 512], kc == 0, kc == 7,
                   ["aw%d" % slot, nk(kc, tb)], prow=(0, ncols))
            return b

        def headnorm_rstd(b):
            q = sqslot()
            act(SQ[:, q, :], PS[b][:, :], AF.Square, ["ps%d" % b], ["sq%d" % q])
            b2 = psbank()
            mm(b2, 512, blk64, SQ[:, q, :], True, True, ["sq%d" % q, "cmat"])
            return rstd_from_psum(b2, 512)

        def proj_qk_norm(chunk, gcolumn, dst, dkey):
            slot = load_w(chunk)
            for tb in range(4):
                b = proj_fm(slot, tb)
                t = headnorm_rstd(b)
                stt(dst[:, tb * 512:(tb + 1) * 512], PS[b][:, :], vec[:, gcolumn:gcolumn + 1], TMP[:, t, :], ALU.mult, ALU.mult,
                    ["ps%d" % b, "tmp%d" % t, "vec"], ["%s_%d" % (dkey, tb)])

        def proj_v_tm(chunk, c0, ncols, dst_fn, dkey):
            slot = load_w(chunk)
            per = 512 // ncols
            for t0 in range(0, 16, per):
                b = psbank()
                for i in range(per):
                    tt_ = t0 + i
                    for kc in range(8):
                        mm(b, ncols, xn[:, kc, tt_ * 128:(tt_ + 1) * 128], AW[:, slot, kc * 128 + c0:kc * 128 + c0 + ncols],
                           kc == 0, kc == 7, ["aw%d" % slot, nk(kc, tt_ // 4)], col0=i * ncols)
                for i in range(per):
                    tt_ = t0 + i
                    o = dst_fn(tt_)
                    src = PS[b][:, i * ncols:(i + 1) * ncols]
                    if i % 2 == 0:
                        S.add("dve", lambda e, o=o, src=src: e.tensor_copy(out=o, in_=src), reads=["ps%d" % b],
                              writes=["%s_%d" % (dkey, tt_ // 4)])
                    else:
                        act(o, src, AF.Copy, ["ps%d" % b], ["%s_%d" % (dkey, tt_ // 4)])

        qslot = lambda i: R[:, i * 2048:(i + 1) * 2048]
        kslot = lambda i: R[:, 4096 + i * 2048:4096 + (i + 1) * 2048]
        vslot = lambda i: R[:, 8192 + i * 3072:8192 + (i + 1) * 3072].rearrange("p (t c) -> p t c", c=192)
        eslot = lambda i: R[:, 14336 + i * 512:14336 + (i + 1) * 512]
        pslot = lambda i: R[:, 16384 + i * 512:16384 + (i + 1) * 512]
        epi = [0, 0]

        def attn_core(qs, ks, vkey, blocks, mask_fn, pv, fin, accsets, skip_fn=None, nsp=2, lay=None):
            qT = qslot(qs); kT = kslot(ks)
            flat = []
            for bi, (q0, nq, kcs, tabsel) in enumerate(blocks):
                kl = [kc for kc in kcs if not (skip_fn is not None and skip_fn(q0, kc, nq))]
                for i, kc in enumerate(kl):
                    flat.append((bi, q0, nq, kc, tabsel, i == 0, i == len(kl) - 1))
            sb_of = {}

            def qk(n):
                bi, q0, nq, kc, tabsel, first, last = flat[n]
                sbk = [2 * (n % nsp), 2 * (n % nsp) + 1]
                for side in range(2):
                    if lay is None:
                        pr = slice(side * 64, (side + 1) * 64)
                        mm(sbk[side], nq, kT[pr, kc * 128:(kc + 1) * 128], qT[pr, q0:q0 + nq], True, True,
                           ["k%d_%d" % (ks, kc // 4), "q%d_%d" % (qs, q0 // 512)])
                    else:
                        mm(sbk[side], nq, lay["k"][:, kc * 128:(kc + 1) * 128], lay["q"][side][:, q0:q0 + nq], True, True,
                           ["k%d_%d" % (ks, kc // 4), "q%d_%d" % (qs, q0 // 512)])
                sb_of[n] = sbk
            LA = nsp
            for n0 in range(min(LA, len(flat))):
                qk(n0)
            pend = []
            PSL = (16384, 17408, 19456) if lay is None else lay["P"]
            ESL = (14336, 15360, 18432) if lay is None else lay["E"]
            for n in range(len(flat)):
                bi, q0, nq, kc, tabsel, first, last = flat[n]
                banks = accsets[bi % len(accsets)]
                sbk = sb_of.pop(n)
                pp = epi[1] % len(PSL); epi[1] += 1
                PP = R[:, PSL[pp]:PSL[pp] + 1024].rearrange("p (s q) -> p s q", s=2)
                SS = PSA[:, sbk[0] * 512:(sbk[0] + 2) * 512].rearrange("p (s q) -> p s q", s=2)
                skeys = ["ps%d" % sbk[0], "ps%d" % sbk[1]]
                if mask_fn is None:
                    act(PP[:, :, 0:nq], SS[:, :, 0:nq], AF.Exp, skeys, ["pp%d" % pp])
                else:
                    ep = epi[0] % len(ESL); epi[0] += 1
                    EE = R[:, ESL[ep]:ESL[ep] + 1024].rearrange("p (s q) -> p s q", s=2)
                    act(EE[:, :, 0:nq], SS[:, :, 0:nq], AF.Exp, skeys, ["ee%d" % ep])
                    tt("dve", PP[:, :, 0:nq], EE[:, :, 0:nq], mask_fn(tabsel, q0, kc, nq), ALU.mult, ["ee%d" % ep, "mask"], ["pp%d" % pp])
                used_fb = False
                if pend:
                    for f_ in pend.pop(0):
                        used_fb = bool(f_(sbk[0])) or used_fb
                if last:
                    while pend:
                        for f_ in pend.pop(0):
                            f_(sbk[0])
                if n + LA < len(flat) and not used_fb:
                    qk(n + LA)
                for side in range(2):
                    for (ai, lfn, xr) in pv[side]:
                        mm(banks[ai], nq, lfn(kc), PP[:, side, 0:nq], first, last,
                           ["pp%d" % pp, "%s_%d" % (vkey, kc // 4)] + xr)
                if n + LA < len(flat) and used_fb:
                    qk(n + LA)
                if last:
                    stages = fin(q0, nq, banks, sbk[0])
                    for i_, stg_ in enumerate(stages):
                        if i_ < len(pend):
                            pend[i_].extend(stg_)
                        else:
                            pend.append(list(stg_))
            for lst_ in pend:
                for f_ in lst_:
                    f_(0)

        def out_proj(c_out0, half, nkc=4, chain=None):
            pending = []
            for oc in range(8):
                slot = load_w(c_out0 + oc)
                for tb in range(4):
                    b = psbank()
                    for kc in range(nkc):
                        kk = half * 4 + kc
                        mm(b, 512, AW[:, slot, kk * 128:(kk + 1) * 128], AH[:, kc, tb * 512:(tb + 1) * 512], kc == 0, kc == nkc - 1,
                           ["aw%d" % slot, "ah%d_%d" % (kc, tb)])
                    stt(xT[:, oc, tb * 512:(tb + 1) * 512], PS[b][:, :], 1.0, xT[:, oc, tb * 512:(tb + 1) * 512], ALU.mult, ALU.add,
                        ["ps%d" % b, xk(oc, tb)], [xk(oc, tb)])
                    if oc == 7 and chain is not None:
                        pending.append(chain(tb))
                        for stg_ in pending:
                            if stg_:
                                stg_.pop(0)()
            while any(pending):
                for stg_ in pending:
                    if stg_:
                        stg_.pop(0)()

        def recip(t, nq, scr):
            S.add("dve", lambda e, t=t, nq=nq: e.reciprocal(out=TMP[:, t, 0:nq], in_=TMP[:, t, 0:nq]),
                  reads=["tmp%d" % t], writes=["tmp%d" % t])

        def fin_pair(cj):
            def fin(q0, nq, banks, fb):
                t = tmp()
                tbq = q0 // 512

                def s0(fb_):
                    act(TMP[64:128, t, 0:nq], PS[banks[0]][64:128, 0:nq], AF.Copy, ["ps%d" % banks[0]], ["tmp%d" % t])
                    act(TMP[0:64, t, 0:nq], PS[banks[1]][0:64, 0:nq], AF.Copy, ["ps%d" % banks[1]], ["tmp%d" % t])

                def s1(fb_):
                    recip(t, nq, None)

                def s2(fb_):
                    tt("dve", AH[0:64, cj, q0:q0 + nq], PS[banks[0]][0:64, 0:nq], TMP[64:128, t, 0:nq], ALU.mult,
                       ["ps%d" % banks[0], "tmp%d" % t], ["ah%d_%d" % (cj, tbq)])
                    tt("dve", AH[64:128, cj, q0:q0 + nq], PS[banks[1]][64:128, 0:nq], TMP[0:64, t, 0:nq], ALU.mult,
                       ["ps%d" % banks[1], "tmp%d" % t], ["ah%d_%d" % (cj, tbq)])
                return [[s0], [s1], [s2]]
            return fin

        def fin_pair1(cj, act_recip=False):
            def fin(q0, nq, banks, fb):
                tr, to = tmp(), tmp()
                tbq = q0 // 512
                f0 = AF.Ln if act_recip else AF.Copy

                def s0(fb_):
                    act(TMP[0:64, tr, 0:nq], PS[banks[0]][64:128, 0:nq], f0, ["ps%d" % banks[0]], ["tmp%d" % tr])
                    act(TMP[64:128, tr, 0:nq], PS[banks[1]][0:64, 0:nq], f0, ["ps%d" % banks[1]], ["tmp%d" % tr])
                    S.add("dve", lambda e: e.tensor_copy(out=TMP[0:64, to, 0:nq], in_=PS[banks[0]][0:64, 0:nq]),
                          reads=["ps%d" % banks[0]], writes=["tmp%d" % to])
                    S.add("dve", lambda e: e.tensor_copy(out=TMP[64:128, to, 0:nq], in_=PS[banks[1]][64:128, 0:nq]),
                          reads=["ps%d" % banks[1]], writes=["tmp%d" % to])

                def s1(fb_):
                    if act_recip:
                        act(TMP[:, tr, 0:nq], TMP[:, tr, 0:nq], AF.Exp, ["tmp%d" % tr], ["tmp%d" % tr], scale=-1.0)
                    else:
                        recip(tr, nq, None)

                def s2(fb_):
                    tt("dve", AH[:, cj, q0:q0 + nq], TMP[:, to, 0:nq], TMP[:, tr, 0:nq], ALU.mult,
                       ["tmp%d" % to, "tmp%d" % tr], ["ah%d_%d" % (cj, tbq)])
                return [[s0], [s1], [s2]]
            return fin

        FULLB = [(qb * 512, 512, list(range(16)), 0) for qb in range(4)]

        def phase_mix0(chain=None):
            RK = ["R"]
            S.fence(RK + ["gu0", "gu1", "dw0", "dw1"] + ["act%d_%d_%d" % (s_, j, tb) for s_ in range(2) for j in range(2) for tb in range(4)])
            for h in range(4):
                qs = ks = vs = h % 2
                proj_qk_norm(C_AB_IN + h, V_GQ8 + 0, qslot(qs), "q%d" % qs)
                proj_qk_norm(C_AB_IN + 4 + h, V_AK, kslot(ks), "k%d" % ks)
                proj_v_tm(C_AB_IN + 8 + h, 0, 128, lambda tt_, vs=vs:
                          R[:, 8192 + vs * 3072 + tt_ * 128:8192 + vs * 3072 + (tt_ + 1) * 128], "v%d" % vs)
                dma("sp", Dtab, alibi_d[h * 128:(h + 1) * 128, :], ["mask"], ["mask"], "dtab")

                def mask_fn(tabsel, q0, kc, nq):
                    m0 = q0 - 128 * kc + 1920
                    return Dtab[:, m0:m0 + nq].unsqueeze(1).broadcast_to([128, 2, nq])
                vl = lambda kc, vs=vs: R[:, 8192 + vs * 3072 + kc * 128:8192 + vs * 3072 + (kc + 1) * 128]
                ol = lambda kc: ones1
                pv = [[(0, vl, []), (1, ol, ["cmat"])], [(2, vl, []), (3, ol, ["cmat"])]]

                def fin(q0, nq, banks, fb, h=h):
                    tbq = q0 // 512
                    tz1, tz2, to1, to2, tr = tmp(), tmp(), tmp(), tmp(), tmp()

                    def s0(fb_):
                        for t_, b_ in ((tz1, banks[1]), (tz2, banks[3])):
                            act(TMP[:, t_, :], PS[b_][:, :], AF.Ln, ["ps%d" % b_], ["tmp%d" % t_])
                        for t_, b_ in ((to1, banks[0]), (to2, banks[2])):
                            S.add("dve", lambda e, t_=t_, b_=b_: e.tensor_copy(out=TMP[:, t_, :], in_=PS[b_][:, :]),
                                  reads=["ps%d" % b_], writes=["tmp%d" % t_])

                    def s1(fb_):
                        act(TMP[:, tz1, :], TMP[:, tz1, :], AF.Exp, ["tmp%d" % tz1], ["tmp%d" % tz1], scale=-1.0)

                    def s2(fb_):
                        act(TMP[:, tz2, :], TMP[:, tz2, :], AF.Exp, ["tmp%d" % tz2], ["tmp%d" % tz2], scale=-1.0)

                    def s3(fb_):
                        tt("dve", TMP[:, to1, :], TMP[:, to1, :], TMP[:, tz1, :], ALU.mult, ["tmp%d" % to1, "tmp%d" % tz1], ["tmp%d" % to1])
                        tt("dve", TMP[:, to2, :], TMP[:, to2, :], TMP[:, tz2, :], ALU.mult, ["tmp%d" % to2, "tmp%d" % tz2], ["tmp%d" % to2])

                    sqq = [0]

                    def s4(fb_):
                        stt(TMP[:, to1, :], TMP[:, to2, :], neglam, TMP[:, to1, :], ALU.mult, ALU.add,
                            ["tmp%d" % to1, "tmp%d" % to2, "lam2"], ["tmp%d" % to1])
                        sqq[0] = sqslot()
                        act(SQ[:, sqq[0], :], TMP[:, to1, :], AF.Square, ["tmp%d" % to1], ["sq%d" % sqq[0]])

                    def s5(fb_):
                        mm(fb_, 512, ones128, SQ[:, sqq[0], :], True, True, ["sq%d" % sqq[0], "cmat"])
                        act(TMP[:, tr, :], PS[fb_][:, :], AF.Ln, ["ps%d" % fb_, "vec"], ["tmp%d" % tr], bias=eps_ap, scale=1.0)
                        act(TMP[:, tr, :], TMP[:, tr, :], AF.Exp, ["tmp%d" % tr], ["tmp%d" % tr], scale=-0.5)
                        return True

                    def s6(fb_):
                        stt(AH[:, h, q0:q0 + 512], TMP[:, to1, :], vec[:, V_SUB8:V_SUB8 + 1], TMP[:, tr, :], ALU.mult, ALU.mult,
                            ["tmp%d" % to1, "tmp%d" % tr, "vec"], ["ah%d_%d" % (h, tbq)])
                    return [[s0], [s1], [s2], [s3], [s4], [s5], [s6]]
                attn_core(qs, ks, "v%d" % vs, FULLB, mask_fn, pv, fin, [[4, 5, 6, 7]],
                          skip_fn=(None if os.environ.get("NOSKIP") else (lambda q0, kc, nq, h=h: bool(ALIBI_ZERO[h][q0 // 512][kc]))))
            out_proj(C_AB_OUT, 0)
            NAB = [(0, 256, [0, 1, 2, 3], 0), (256, 256, [0, 1, 2, 3, 4, 5], 1), (512, 512, list(range(2, 10)), 1),
                   (1024, 512, list(range(6, 14)), 1), (1536, 256, list(range(10, 16)), 1), (1792, 256, [12, 13, 14, 15], 0)]
            S.fence(["q%d_%d" % (i, t) for i in range(2) for t in range(4)] + ["k%d_%d" % (i, t) for i in range(2) for t in range(4)] +
                    ["v%d_%d" % (i, t) for i in range(2) for t in range(4)] + ["ee0", "ee1", "ee2", "pp0", "pp1", "pp2"])
            for i in range(2):
                S.add("pool", lambda e, i=i: e.memset(vslot(i)[:, :, 64:128], 1.0), reads=[], writes=["v%d_%d" % (i, t4) for t4 in range(4)])
            for j in range(4):
                qs = ks = vs = j % 2
                proj_qk_norm(C_AB_IN + 12 + j, V_GQ8 + 1, qslot(qs), "q%d" % qs)
                proj_qk_norm(C_AB_IN + 16 + j, V_BK, kslot(ks), "k%d" % ks)
                slot = load_w(C_AB_IN + 20 + j)
                for t0 in range(0, 16, 4):
                    b = psbank()
                    for i in range(4):
                        tt_ = t0 + i
                        for kc in range(8):
                            mm(b, 128, xn[:, kc, tt_ * 128:(tt_ + 1) * 128], AW[:, slot, kc * 128:(kc + 1) * 128], kc == 0, kc == 7,
                               ["aw%d" % slot, nk(kc, tt_ // 4)], col0=i * 128)
                    src = PS[b][:, :].rearrange("p (t h c) -> p t h c", t=4, h=2)
                    V = vslot(vs)
                    S.add("dve", lambda e, V=V, src=src, t0=t0: e.tensor_copy(out=V[:, t0:t0 + 4, 0:64], in_=src[:, :, 0, :]),
                          reads=["ps%d" % b], writes=["v%d_%d" % (vs, t0 // 4)])
                    act(V[:, t0:t0 + 4, 128:192], src[:, :, 1, :], AF.Copy, ["ps%d" % b, "v%d_%d" % (vs, t0 // 4)], ["v%d_%d" % (vs, t0 // 4)])
                for hh in range(2):
                    dma("sp", NAtab[:, hh * 2 * NA_TW:(hh + 1) * 2 * NA_TW], natab[(2 * j + hh) * 128:(2 * j + hh + 1) * 128, :],
                        ["natab", "mask"], ["mask"], "natb")

                def mask_fn(tabsel, q0, kc, nq):
                    mp0 = q0 // 64 - 2 * kc + 7
                    c0 = tabsel * NA_TW + (mp0 + 3) * 64
                    return NAtab.rearrange("p (s c) -> p s c", s=2)[:, :, c0:c0 + nq]
                V = vslot(vs)
                pv = [[(0, lambda kc, V=V: V[:, kc, 0:128], [])], [(1, lambda kc, V=V: V[:, kc, 64:192], [])]]
                attn_core(qs, ks, "v%d" % vs, NAB, mask_fn, pv, fin_pair1(j, act_recip=True), [[6, 7]], nsp=3)
            out_proj(C_AB_OUT, 1, chain=chain)

        def phase_mix1(chain=None):
            S.fence(["R", "gu0", "gu1", "dw0", "dw1", "mask"] + ["act%d_%d_%d" % (s_, j, tb) for s_ in range(2) for j in range(2) for tb in range(4)])
            dma("sp", TBL[:, 0:4096], rope_d, ["mask"], ["rope"], "rope")
            for i in range(2):
                S.add("pool", lambda e, i=i: e.memset(vslot(i)[:, :, 0:64], 1.0), reads=[], writes=["v%d_%d" % (i, t4) for t4 in range(4)])
                S.add("pool", lambda e, i=i: e.memset(vslot(i)[:, :, 128:192], 1.0), reads=[], writes=["v%d_%d" % (i, t4) for t4 in range(4)])

            def proj_rope(chunk, chunk_sw, gcol, gcol_sw, dst, dkey):
                s1 = load_w(chunk)
                s2 = load_w(chunk_sw)
                for tb in range(4):
                    cols = slice(tb * 512, (tb + 1) * 512)
                    b = proj_fm(s1, tb)
                    bs = proj_fm(s2, tb)
                    t = headnorm_rstd(b)
                    t1 = tmp()
                    stt(TMP[:, t1, :], PS[b][:, :], vec[:, gcol:gcol + 1], ropeC[:, cols], ALU.mult, ALU.mult,
                        ["ps%d" % b, "vec", "rope"], ["tmp%d" % t1])
                    t2 = tmp()
                    stt(TMP[:, t2, :], PS[bs][:, :], vec[:, gcol_sw:gcol_sw + 1], ropeS[:, cols], ALU.mult, ALU.mult,
                        ["ps%d" % bs, "vec", "rope"], ["tmp%d" % t2])
                    tt(os.environ.get("ADDENG", "dve"), TMP[:, t1, :], TMP[:, t1, :], TMP[:, t2, :], ALU.add, ["tmp%d" % t1, "tmp%d" % t2], ["tmp%d" % t1])
                    tt("dve", dst[:, cols], TMP[:, t1, :], TMP[:, t, :], ALU.mult, ["tmp%d" % t1, "tmp%d" % t], ["%s_%d" % (dkey, tb)])
            CUT = int(os.environ.get("MIX1_CUT", "99"))
            if CUT <= 1:
                return
            for j in range(8):
                kv = j // 2
                qs = j % 2
                ks = vs = kv % 2
                if j % 2 == 0:
                    proj_rope(C_CK + kv, C_CKS + kv, V_CK, V_CKS, kslot(ks), "k%d" % ks)
                    if CUT <= 2:
                        return
                    V = vslot(vs)
                    proj_v_tm(C_CV + kv // 2, (kv % 2) * 64, 64, lambda tt_, V=V: V[:, tt_, 64:128], "v%d" % vs)
                    if CUT <= 3:
                        return
                VAR = os.environ.get("VARQ", "")
                if VAR == "A":
                    proj_rope(C_CK + kv, C_CKS + kv, V_CK, V_CKS, kslot(ks), "k%d" % ks)
                elif VAR == "B":
                    proj_rope(C_CQ + j, C_CQS + j, V_CK, V_CKS, qslot(qs), "q%d" % qs)
                elif VAR == "C":
                    proj_rope(C_CQ + j, C_CQS + j, V_GQ8 + 2, V_GQ8 + 3, kslot(ks), "k%d" % ks)
                else:
                    proj_rope(C_CQ + j, C_CQS + j, V_GQ8 + 2, V_GQ8 + 3, qslot(qs), "q%d" % qs)
                if CUT <= 4:
                    return
                V = vslot(vs)
                pv = [[(0, lambda kc, V=V: V[:, kc, 64:192], [])], [(1, lambda kc, V=V: V[:, kc, 0:128], [])]]
                attn_core(qs, ks, "v%d" % vs, FULLB, None, pv, fin_pair1(j % 4), [[6, 7]], nsp=3)
                if j % 4 == 3:
                    out_proj(C_C_OUT, j // 4, chain=(chain if j == 7 else None))

        def ffn_fence():
            S.fence(["R", "mask", "rope", "stg0", "stg1", "stgi0", "stgi1"] + ["q%d_%d" % (i, t) for i in range(2) for t in range(4)] +
                    ["k%d_%d" % (i, t) for i in range(2) for t in range(4)] + ["v%d_%d" % (i, t) for i in range(2) for t in range(4)] +
                    ["ee0", "ee1", "ee2", "pp0", "pp1", "pp2"])

        ACTK = ["act%d_%d_%d" % (s_, j, tb) for s_ in range(2) for j in range(2) for tb in range(4)]
        for s in range(nseq):
            if s == 0:
                ffn_fence()
                phase_in(s)
            if stop < 99:
                ffn_fence()
                for l in range(2):
                    if stop <= 3 * l:
                        break
                    phase_norm(V_FFN1 + 8 * l, False)
                    phase_ffn(l * 2 + 0)
                    if stop <= 3 * l + 1:
                        break
                    phase_norm(V_MIX + 8 * l, False)
                    if l == 0:
                        phase_mix0()
                    else:
                        phase_mix1()
                    if stop <= 3 * l + 2:
                        break
                    ffn_fence()
                    phase_norm(V_FFN2 + 8 * l, False)
                    phase_ffn(l * 2 + 1)
                    phase_norm(V_FIN + 8 * l, True)
            else:
                phase_norm(V_FFN1, False)
                ffn_fence()
                for l in range(2):
                    if s == 0 and l == 0:
                        cast_ffn(1)
                    phase_ffn(l * 2 + 0, chain=norm_chain([(V_MIX + 8 * l, False)]))
                    if s == 0 and l == 0:
                        cast_ffn(2)
                    ch2 = norm_chain([(V_FFN2 + 8 * l, False)])
                    if l == 0:
                        phase_mix0(chain=ch2)
                    else:
                        phase_mix1(chain=ch2)
                    ffn_fence()
                    if s == 0 and l == 0:
                        cast_ffn(3)
                    specs = [(V_FIN + 8 * l, True)] + ([(V_FFN1 + 8, False)] if l == 0 else [])
                    phase_ffn(l * 2 + 1, chain=norm_chain(specs))
            ffn_fence()
            S.fence(["gu0", "gu1", "dw0", "dw1"] + ACTK)
            if s + 1 < nseq and stop >= 99:
                phase_out_in(s)
            else:
                phase_out(s)
                if s + 1 < nseq:
                    phase_in(s + 1)
        S.add("sp", None, reads=["yout0", "yout1"])
        S.emit(st)
    return nc


def _chunk(w, cols):
    blk = w[:, cols]
    return np.ascontiguousarray(blk.reshape(8, 128, 128).transpose(1, 0, 2).reshape(128, 1024))


def _consts():
    identf = np.eye(128, dtype=np.float32)
    cm = np.zeros((128, 512), np.float32)
    cm[:, 0:128] = 1.0 / 1024
    cm[0:64, 128:192] = 1.0 / 64
    cm[64:128, 192:256] = 1.0 / 64
    cm[:, 256:384] = 1.0 / 128
    cm[:, 384:512] = 1.0
    cmat = cm.astype(ml_dtypes.bfloat16)
    p = np.arange(128)[:, None]
    m = np.arange(DW)[None, :]
    dist = np.abs(m - p - 1920).astype(np.float64)
    slopes = 2.0 ** (-8.0 * np.arange(1, 5) / 4)
    alibi = np.concatenate([np.exp(-sl * dist) for sl in slopes], axis=0).astype(np.float32).astype(ml_dtypes.bfloat16)
    f = np.arange(128) % 64
    mm_ = f % 16
    inv_freq = (10000.0 ** (-(np.arange(0, 32, 2, dtype=np.float32)) / 32)).astype(np.float32)
    t = np.arange(S_LEN)
    pos = np.where((f[:, None] // 32) == 0, (t[None, :] // 64), (t[None, :] % 64)).astype(np.float32)
    ang = pos * inv_freq[mm_][:, None]
    sign = np.where((f % 32) < 16, -1.0, 1.0)[:, None]
    rope = np.concatenate([np.cos(ang), sign * np.sin(ang)], axis=1).astype(np.float32)
    pp = np.arange(128)
    half = pp // 64
    ck = pp % 64
    mi = np.arange(NA_NM)
    cq = np.arange(64)
    dr = half[:, None, None] + 7 - (mi[None, :, None] - 3)
    dc = ck[:, None, None] - cq[None, None, :]
    cstart = np.clip(cq - 8, 0, 48)[None, None, :]
    colv = (ck[:, None, None] >= cstart) & (ck[:, None, None] < cstart + 16)
    vF = (np.abs(dr) <= 7) & colv
    vI = (dr >= -4) & (dr <= 3) & colv
    navalid = np.concatenate([vF.reshape(128, -1), vI.reshape(128, -1)], axis=1).astype(np.float32).astype(ml_dtypes.bfloat16)
    dr_b = np.broadcast_to(dr, vF.shape)
    dc_b = np.broadcast_to(dc, vF.shape)
    return identf, cmat, alibi, rope, navalid, vF, dr_b, dc_b


def _prep(inp):
    g = lambda k: np.asarray(inp[k], dtype=np.float32)
    blocks = []
    for l in range(2):
        for nm in ("ffn1", "ffn2"):
            wg = g(nm + "_w_gate")[l].reshape(8, 128, NG, 256).transpose(2, 1, 0, 3).reshape(NG, 128, 2048)
            wu = g(nm + "_w_up")[l].reshape(8, 128, NG, 256).transpose(2, 1, 0, 3).reshape(NG, 128, 2048)
            wd = g(nm + "_w_down")[l].reshape(NG, 2, 128, 1024).transpose(0, 2, 1, 3).reshape(NG, 128, 2048)
            blocks.append(np.concatenate([wg, wu, wd], axis=2))
    wffn = np.ascontiguousarray(np.stack(blocks).reshape(4 * NG * 128, 6144))
    ch = []
    abin = g("ab_w_in")[0]
    for c in range(24):
        ch.append(_chunk(abin, slice(c * 128, (c + 1) * 128)))
    abo = g("ab_w_out")[0]
    for c in range(8):
        ch.append(_chunk(abo, slice(c * 128, (c + 1) * 128)))
    cin = g("c_w_in")[0]
    sw = np.arange(1024) ^ 16
    wq = cin[:, 0:1024]
    for c in range(8):
        ch.append(_chunk(wq, slice(c * 128, (c + 1) * 128)))
    wqs = wq[:, sw]
    for c in range(8):
        ch.append(_chunk(wqs, slice(c * 128, (c + 1) * 128)))
    wk = cin[:, 1024:1280]
    wks = wk[:, np.arange(256) ^ 16]
    for src in (wk, wks):
        for kv in range(4):
            cols = np.concatenate([np.arange(kv * 64, kv * 64 + 64)] * 2)
            ch.append(_chunk(src, cols))
    wv = cin[:, 1280:1536]
    for c in range(2):
        ch.append(_chunk(wv, slice(c * 128, (c + 1) * 128)))
    co = g("c_w_out")[0]
    for c in range(8):
        ch.append(_chunk(co, slice(c * 128, (c + 1) * 128)))
    watt = np.ascontiguousarray(np.concatenate(ch, axis=0))
    assert watt.shape == (N_ATT * 128, 1024)
    vec = np.zeros((128, NVEC), np.float32)
    for l in range(2):
        for base, nm in ((V_FFN1, "ffn1_norm"), (V_MIX, "mix_norm"), (V_FFN2, "ffn2_norm"), (V_FIN, "final_norm")):
            vec[:, base + 8 * l:base + 8 * l + 8] = g(nm)[l].reshape(8, 128).T
    t2 = lambda v: np.concatenate([v, v])
    vec[:, V_AQ] = t2(g("a_q_norm")[0]); vec[:, V_AK] = t2(g("a_k_norm")[0])
    vec[:, V_BQ] = t2(g("b_q_norm")[0]); vec[:, V_BK] = t2(g("b_k_norm")[0])
    vec[:, V_SUB] = g("a_sub_norm")[0]
    s64 = np.arange(64) ^ 16
    vec[:, V_CQ] = t2(g("c_q_norm")[0]); vec[:, V_CQS] = t2(g("c_q_norm")[0][s64])
    vec[:, V_CK] = t2(g("c_k_norm")[0]); vec[:, V_CKS] = t2(g("c_k_norm")[0][s64])
    lam = np.concatenate([g("a_lambda_q1")[0], g("a_lambda_k1")[0], g("a_lambda_q2")[0], g("a_lambda_k2")[0]])
    lamv = np.ascontiguousarray(np.broadcast_to(lam[None, :], (128, 256)))
    identf, cmat, alibi, rope, navalid, vF, dr_b, dc_b = _consts()
    rpb = g("b_rpb")[0]
    rpbt = np.zeros((8,) + vF.shape, np.float32)
    idx = np.nonzero(vF)
    for h in range(8):
        rpbt[h][idx] = rpb[h][dr_b[idx] + 7, dc_b[idx] + 15]
    rpbt = np.ascontiguousarray(rpbt.reshape(8 * 128, NA_TW))
    return dict(wffn=wffn, watt=watt, vec=vec, lamv=lamv, identf=identf, cmat=cmat, alibi=alibi, rope=rope,
                rpbt=rpbt, navalid=navalid)


def kernel(**inputs):
    xp = np.asarray(inputs["x_prompt"], dtype=np.float32)
    xs = np.asarray(inputs["x_sample"], dtype=np.float32)
    shared = _prep(inputs)
    npc, nsc = xp.shape[0] // NCORE, xs.shape[0] // NCORE
    nseq = npc + nsc
    nc = build(nseq)
    in_maps = []
    for i in range(NCORE):
        xi = np.concatenate([xp[i * npc:(i + 1) * npc], xs[i * nsc:(i + 1) * nsc]], axis=0).reshape(nseq * S_LEN, D)
        m = dict(shared)
        m["xin"] = np.ascontiguousarray(xi)
        in_maps.append(m)
    res = run_bass_kernel_spmd(nc, in_maps, core_ids=list(range(NCORE)))
    yp = np.empty_like(xp)
    ys = np.empty_like(xs)
    for i in range(NCORE):
        y = np.asarray(res.results[i]["yout"]).reshape(nseq, S_LEN, D)
        yp[i * npc:(i + 1) * npc] = y[:npc]
        ys[i * nsc:(i + 1) * nsc] = y[npc:]
    return (yp, ys)
```
